# Optimizing a Trainium2 kernel written in Bass

```python
import math
import jax, jax.numpy as jnp
from jax import lax
import numpy as np

D_MODEL = 2048
BATCH = 1
SEQ = 16384
DEPTH = 1
DEC_BATCH = 16
DEC_SEQ = 64
PAST_LEN = 4096

CHUNK = 64
EPS = 1e-6
D_CONV = D_MODEL
CONV_A_WIDTH = 31
SSD_EXPAND = 2
D_INNER = SSD_EXPAND * D_MODEL
SSD_HEAD_DIM = 64
SSD_HEADS = D_INNER // SSD_HEAD_DIM
SSD_GROUPS = 8
SSD_STATE = 128
D_XBC = D_INNER + 2 * SSD_GROUPS * SSD_STATE
CONV_B_WIDTH = 4
D_FF = 5632
CONV_F_WIDTH = 3
N_BRANCHES = 2
SPLIT_POINTS = (
    D_CONV,
    2 * D_CONV,
    2 * D_CONV + D_INNER,
    2 * D_CONV + D_INNER + D_XBC,
    2 * D_CONV + D_INNER + D_XBC + SSD_HEADS,
)
D_IN_PROJ = 2 * D_CONV + D_INNER + D_XBC + SSD_HEADS + N_BRANCHES * D_MODEL

kernel_name = "hybrid_conformer_ssd_streaming_step"


def rms_norm(x, g):
    x32 = x.astype(jnp.float32)
    y = x32 * lax.rsqrt(jnp.mean(x32 * x32, axis=-1, keepdims=True) + EPS)
    return (y * g.astype(jnp.float32)).astype(x.dtype)


def layer_norm(x, g, b):
    x32 = x.astype(jnp.float32)
    mu = jnp.mean(x32, axis=-1, keepdims=True)
    xc = x32 - mu
    var = jnp.mean(xc * xc, axis=-1, keepdims=True)
    y = xc * lax.rsqrt(var + EPS)
    return (y * g.astype(jnp.float32) + b.astype(jnp.float32)).astype(x.dtype)


def causal_dwconv(x, buf, w, b):
    width = w.shape[0]
    xp = jnp.concatenate([buf.astype(x.dtype), x], axis=1)
    y = lax.conv_general_dilated(
        xp, w[:, None, :].astype(x.dtype), window_strides=(1,), padding='VALID',
        dimension_numbers=('NWC', 'WIO', 'NWC'), feature_group_count=x.shape[-1])
    return y + b.astype(x.dtype), xp[:, xp.shape[1] - (width - 1):]


def ssd_scan(x, dt, a, bmat, cmat, state0, chunk):
    f32 = jnp.float32
    bsz, L, H, P = x.shape
    G, N = bmat.shape[2], bmat.shape[3]
    hg = H // G
    nc = L // chunk
    xc = x.astype(f32).reshape(bsz, nc, chunk, G, hg, P).transpose(1, 0, 2, 3, 4, 5)
    dtc = dt.astype(f32).reshape(bsz, nc, chunk, G, hg).transpose(1, 0, 2, 3, 4)
    bc = bmat.astype(f32).reshape(bsz, nc, chunk, G, N).transpose(1, 0, 2, 3, 4)
    cc = cmat.astype(f32).reshape(bsz, nc, chunk, G, N).transpose(1, 0, 2, 3, 4)
    af = a.astype(f32).reshape(G, hg)
    causal = jnp.tril(jnp.ones((chunk, chunk), dtype=bool))[None, :, :, None, None]

    def step(state, inp):
        xk, dtk, bk, ck = inp
        cum = jnp.cumsum(dtk * af, axis=1)
        seg = cum[:, :, None] - cum[:, None, :]
        decay = jnp.exp(jnp.where(causal, seg, -jnp.inf))
        scores = jnp.einsum('bign,bjgn->bijg', ck, bk)
        y = jnp.einsum('bijg,bijgh,bjgh,bjghp->bighp', scores, decay, dtk, xk)
        y = y + jnp.einsum('bign,bghpn,bigh->bighp', ck, state, jnp.exp(cum))
        last = cum[:, -1]
        wgt = dtk * jnp.exp(last[:, None] - cum)
        state = state * jnp.exp(last)[..., None, None] + jnp.einsum('bjgn,bjgh,bjghp->bghpn', bk, wgt, xk)
        return state, y

    state, ys = lax.scan(step, state0.astype(f32).reshape(bsz, G, hg, P, N), (xc, dtc, bc, cc))
    y = ys.transpose(1, 0, 2, 3, 4, 5).reshape(bsz, L, H, P)
    return y.astype(x.dtype), state.reshape(bsz, H, P, N).astype(state0.dtype)


def trunk_layer(x, buf_a, buf_b, ssd_state, buf_f,
                norm_mix_pre, w_in, b_gate, conv_a_w, conv_a_b, ln_a_g, ln_a_b, w_a_out,
                conv_b_w, conv_b_b, dt_bias, a_log, d_skip, ssd_norm_g, w_b_out, w_o, norm_mix_post,
                norm_ffn_pre, w_up, ffn_conv_w, ffn_conv_b, w_down, norm_ffn_post):
    bsz, L, _ = x.shape
    chunk = CHUNK if L % CHUNK == 0 else L
    h = rms_norm(x, norm_mix_pre)
    proj = h @ w_in
    a_val, a_gate, z, xbc, dt_raw, gate_logits = jnp.split(proj, SPLIT_POINTS, axis=-1)
    gates = jax.nn.sigmoid(gate_logits + b_gate)
    g_a, g_b = jnp.split(gates, N_BRANCHES, axis=-1)
    u = a_val * jax.nn.sigmoid(a_gate)
    u, new_buf_a = causal_dwconv(u, buf_a, conv_a_w, conv_a_b)
    u = jax.nn.silu(layer_norm(u, ln_a_g, ln_a_b))
    y_a = u @ w_a_out
    xbc, new_buf_b = causal_dwconv(xbc, buf_b, conv_b_w, conv_b_b)
    xbc = jax.nn.silu(xbc)
    xs, bm, cm = jnp.split(xbc, (D_INNER, D_INNER + SSD_GROUPS * SSD_STATE), axis=-1)
    xs = xs.reshape(bsz, L, SSD_HEADS, SSD_HEAD_DIM)
    bm = bm.reshape(bsz, L, SSD_GROUPS, SSD_STATE)
    cm = cm.reshape(bsz, L, SSD_GROUPS, SSD_STATE)
    dt = jax.nn.softplus((dt_raw + dt_bias).astype(jnp.float32))
    a = -jnp.exp(a_log.astype(jnp.float32))
    ys, new_state = ssd_scan(xs, dt, a, bm, cm, ssd_state, chunk)
    ys = (ys + d_skip[:, None] * xs).reshape(bsz, L, D_INNER)
    ys = rms_norm(ys * jax.nn.silu(z), ssd_norm_g)
    y_b = ys @ w_b_out
    merged = g_a * y_a + g_b * y_b
    x = x + rms_norm(merged @ w_o, norm_mix_post)
    h = rms_norm(x, norm_ffn_pre)
    up = h @ w_up
    up, new_buf_f = causal_dwconv(up, buf_f, ffn_conv_w, ffn_conv_b)
    u_gate, u_val = jnp.split(up, 2, axis=-1)
    x = x + rms_norm((jax.nn.gelu(u_gate) * u_val) @ w_down, norm_ffn_post)
    return x, new_buf_a, new_buf_b, new_state, new_buf_f


def setup_inputs(seed: int = 0) -> dict:
    key = jax.random.key(seed)
    ks = jax.random.split(key, 32)
    f32 = jnp.float32

    def nrm(k, shape, scale):
        return jax.random.normal(k, shape, f32) * scale

    def gain(k, n):
        return 1.0 + nrm(k, (DEPTH, n), 0.02)

    dt0 = jnp.exp(jax.random.uniform(ks[20], (DEPTH, SSD_HEADS), f32, math.log(1e-3), math.log(1e-1)))
    return {
        "x_prompt": nrm(ks[0], (BATCH, SEQ, D_MODEL), 1.0),
        "x_sample": nrm(ks[1], (DEC_BATCH, DEC_SEQ, D_MODEL), 1.0),
        "cache_conv_a": nrm(ks[2], (DEPTH, DEC_BATCH, CONV_A_WIDTH - 1, D_CONV), 0.5),
        "cache_conv_b": nrm(ks[3], (DEPTH, DEC_BATCH, CONV_B_WIDTH - 1, D_XBC), 1.0),
        "state_ssd": nrm(ks[4], (DEPTH, DEC_BATCH, SSD_HEADS, SSD_HEAD_DIM, SSD_STATE), 0.1),
        "cache_ffn_conv": nrm(ks[5], (DEPTH, DEC_BATCH, CONV_F_WIDTH - 1, 2 * D_FF), 1.0),
        "norm_mix_pre": gain(ks[6], D_MODEL),
        "w_in": nrm(ks[7], (DEPTH, D_MODEL, D_IN_PROJ), D_MODEL ** -0.5),
        "b_gate": nrm(ks[8], (DEPTH, N_BRANCHES * D_MODEL), 0.02),
        "conv_a_w": nrm(ks[9], (DEPTH, CONV_A_WIDTH, D_CONV), CONV_A_WIDTH ** -0.5),
        "conv_a_b": nrm(ks[10], (DEPTH, D_CONV), 0.02),
        "ln_a_g": gain(ks[11], D_CONV),
        "ln_a_b": nrm(ks[12], (DEPTH, D_CONV), 0.02),
        "w_a_out": nrm(ks[13], (DEPTH, D_CONV, D_MODEL), D_CONV ** -0.5),
        "conv_b_w": nrm(ks[14], (DEPTH, CONV_B_WIDTH, D_XBC), CONV_B_WIDTH ** -0.5),
        "conv_b_b": nrm(ks[15], (DEPTH, D_XBC), 0.02),
        "dt_bias": dt0 + jnp.log(-jnp.expm1(-dt0)),
        "a_log": jnp.log(jax.random.uniform(ks[16], (DEPTH, SSD_HEADS), f32, 1.0, 16.0)),
        "d_skip": gain(ks[17], SSD_HEADS),
        "ssd_norm_g": gain(ks[18], D_INNER),
        "w_b_out": nrm(ks[19], (DEPTH, D_INNER, D_MODEL), D_INNER ** -0.5),
        "w_o": nrm(ks[21], (DEPTH, D_MODEL, D_MODEL), D_MODEL ** -0.5),
        "norm_mix_post": gain(ks[22], D_MODEL),
        "norm_ffn_pre": gain(ks[23], D_MODEL),
        "w_up": nrm(ks[24], (DEPTH, D_MODEL, 2 * D_FF), D_MODEL ** -0.5),
        "ffn_conv_w": nrm(ks[25], (DEPTH, CONV_F_WIDTH, 2 * D_FF), CONV_F_WIDTH ** -0.5),
        "ffn_conv_b": nrm(ks[26], (DEPTH, 2 * D_FF), 0.02),
        "w_down": nrm(ks[27], (DEPTH, D_FF, D_MODEL), D_FF ** -0.5),
        "norm_ffn_post": gain(ks[28], D_MODEL),
    }


def reference(x_prompt, x_sample, cache_conv_a, cache_conv_b, state_ssd, cache_ffn_conv,
              norm_mix_pre, w_in, b_gate, conv_a_w, conv_a_b, ln_a_g, ln_a_b, w_a_out,
              conv_b_w, conv_b_b, dt_bias, a_log, d_skip, ssd_norm_g, w_b_out, w_o, norm_mix_post,
              norm_ffn_pre, w_up, ffn_conv_w, ffn_conv_b, w_down, norm_ffn_post):
    weights = (norm_mix_pre, w_in, b_gate, conv_a_w, conv_a_b, ln_a_g, ln_a_b, w_a_out,
               conv_b_w, conv_b_b, dt_bias, a_log, d_skip, ssd_norm_g, w_b_out, w_o, norm_mix_post,
               norm_ffn_pre, w_up, ffn_conv_w, ffn_conv_b, w_down, norm_ffn_post)
    bp = x_prompt.shape[0]
    dtype = x_prompt.dtype
    yp, ys = x_prompt, x_sample
    st_p = ([], [], [], [])
    st_s = ([], [], [], [])
    for layer in range(DEPTH):
        wl = [w[layer] for w in weights]
        zero_a = jnp.zeros((bp, CONV_A_WIDTH - 1, D_CONV), dtype)
        zero_b = jnp.zeros((bp, CONV_B_WIDTH - 1, D_XBC), dtype)
        zero_s = jnp.zeros((bp, SSD_HEADS, SSD_HEAD_DIM, SSD_STATE), dtype)
        zero_f = jnp.zeros((bp, CONV_F_WIDTH - 1, 2 * D_FF), dtype)
        yp, pa, pb, ps, pf = trunk_layer(yp, zero_a, zero_b, zero_s, zero_f, *wl)
        ys, sa, sb, ss, sf = trunk_layer(ys, cache_conv_a[layer], cache_conv_b[layer],
                                         state_ssd[layer], cache_ffn_conv[layer], *wl)
        for lst, v in zip(st_p, (pa, pb, ps, pf)):
            lst.append(v)
        for lst, v in zip(st_s, (sa, sb, ss, sf)):
            lst.append(v)
    new_conv_a_prompt = jnp.stack(st_p[0])
    new_conv_b_prompt = jnp.stack(st_p[1])
    new_ssd_prompt = jnp.stack(st_p[2])
    new_ffn_conv_prompt = jnp.stack(st_p[3])
    new_conv_a_sample = jnp.stack(st_s[0])
    new_conv_b_sample = jnp.stack(st_s[1])
    new_ssd_sample = jnp.stack(st_s[2])
    new_ffn_conv_sample = jnp.stack(st_s[3])
    return (yp, ys, new_conv_a_prompt, new_conv_b_prompt, new_ssd_prompt, new_ffn_conv_prompt,
            new_conv_a_sample, new_conv_b_sample, new_ssd_sample, new_ffn_conv_sample)
```

```python
import contextlib
import os
STOP = float(os.environ.get("KSTOP", "99"))
SKIPY = os.environ.get("KSKIPY", "0") == "1"
import numpy as np
import concourse.bass as bass
import concourse.mybir as mybir
from concourse.bass_utils import run_bass_kernel_spmd

F32 = mybir.dt.float32
BF16 = mybir.dt.bfloat16
AF = mybir.ActivationFunctionType
ALU = mybir.AluOpType

D = 2048
KC = 16
DI = 4096
DXBC = 6144
NH = 64
DFF = 5632
NJ = 44
EPS = 1e-6
C_AVAL, C_AGATE, C_Z, C_XBC, C_DT, C_GA, C_GB = 0, 2048, 4096, 8192, 14336, 14400, 16448
DINP = 18496
T = 256
NS = 4
NSUB = 2
L = 64
NEG = -30000.0

PF = {}
_o = 0
for _n, _w in [("npre", 16), ("caw", 16 * 31), ("cab", 16), ("lng", 16), ("lnb", 16), ("bg", 32),
               ("cbw", 48 * 4), ("cbb", 48), ("sng", 32), ("nfpre", 16), ("fcw", 88 * 3), ("fcb", 88),
               ("dsk", 32), ("dtb", 1), ("alog", 1), ("mk", 1), ("sel", 64), ("selm", 8)]:
    PF[_n] = (_o, _w)
    _o += _w
NPF = _o
CS = {"ident": (0, 128), "U": (128, 64), "negm": (192, 64), "ones": (256, 128)}
NCS = 384


class Sched:
    def __init__(self):
        self.streams = {e: [] for e in ("pe", "act", "dve", "pool", "sp")}
        self.cnt = {e: 0 for e in self.streams}
        self.seen = {e: {} for e in self.streams}
        self.res_w = {}
        self.res_r = {}
        self.dcnt = {}
        self.rotc = {}

    def rot(self, name, n):
        i = self.rotc.get(name, 0)
        self.rotc[name] = i + 1
        return i % n

    def op(self, eng, fn, reads=(), writes=(), inc=True, dma=None):
        reads = list(reads)
        writes = list(writes)
        ps_reads = [r for r in reads if isinstance(r, tuple) and len(r) == 2 and r[0] == "ps"]
        reads = [r for r in reads if r not in ps_reads]
        for r in ps_reads:
            if r not in writes:
                writes.append(r)
        waits = {}

        def need(t):
            if t is None:
                return
            k, v = t
            if k == "pe" and eng == "pe":
                return
            if self.seen[eng].get(k, 0) >= v:
                return
            waits[k] = max(waits.get(k, 0), v)

        for r in reads:
            need(self.res_w.get(r))
        for w in writes:
            need(self.res_w.get(w))
            for k, v in self.res_r.get(w, {}).items():
                need((k, v))
        for k, v in waits.items():
            self.seen[eng][k] = v
        if dma is not None:
            self.dcnt[dma] = self.dcnt.get(dma, 0) + 16
            tick = (("d", dma), self.dcnt[dma])
            inc = False
        elif inc:
            self.cnt[eng] += 1
            tick = (eng, self.cnt[eng])
        else:
            assert eng == "pe"
            tick = (eng, self.cnt[eng] + 1)
        for r in reads:
            d = self.res_r.setdefault(r, {})
            d[tick[0]] = max(d.get(tick[0], 0), tick[1])
        for w in writes:
            self.res_w[w] = tick
            self.res_r[w] = {}
        self.streams[eng].append((list(waits.items()), fn, inc, dma))


def build(n_own_mt, with_pass1, n_cores):
    nc = bass.Bass("TRN2", target_bir_lowering=False)
    NMT = 1 + n_own_mt
    NTOK = NMT * T
    NOWN = n_own_mt * T
    S = Sched()
    es = contextlib.ExitStack()

    def din(name, shape, dt=F32):
        return nc.dram_tensor(name, list(shape), dt, kind="ExternalInput").ap()

    def dout(name, shape, dt=F32):
        return nc.dram_tensor(name, list(shape), dt, kind="ExternalOutput").ap()

    xin = din("xin", [NTOK, D])
    x1in = din("x1in", [(n_own_mt + 1) * T, D]) if with_pass1 else None
    pf_d = din("pf", [128, NPF])
    cs_d = din("cs", [128, NCS])
    gam1_d = din("gam1", [128, D])
    gam2_d = din("gam2", [128, D])
    uh_d = din("uh", [2, 128, 16 * 30])
    xh_d = din("xh", [2, 128, 48 * 3])
    fh_d = din("fh", [2, 128, 88 * 2])
    st_d = din("st", [2, 128, DI])
    w_in_d = din("w_in_t", [145, 128, KC * 128])
    w_a_d = din("w_a_t", [16, 128, KC * 128])
    w_b_d = din("w_b_t", [16, 128, 32 * 128])
    w_o_d = din("w_o_t", [16, 128, 4 * 512])
    w_up_d = din("w_up_t", [88, 128, KC * 128])
    w_dn_d = din("w_dn_t", [44, 128, 4 * 512])
    yout = dout("yout", [NTOK, D])
    ouh = dout("ouh", [3, 128, 16 * 30])
    oxh = dout("oxh", [3, 128, 48 * 3])
    ofh = dout("ofh", [3, 128, 88 * 2])
    ost = dout("ost", [3, 128, DI])
    if with_pass1:
        cc_in = nc.dram_tensor("cc_in", [128, DI + NH], F32)
        cc_out = nc.dram_tensor("cc_out", [n_cores * 128, DI + NH], F32)

    def sb(name, f, dt=F32):
        return es.enter_context(nc.sbuf_tensor(name, [128, f], dt))

    hT = sb("hT", KC * T, BF16)
    RB = sb("RB", 48 * T, BF16)
    mrg = sb("mrg", KC * T, BF16)
    xmid = sb("xmid", NSUB * D)
    WS = [sb(f"ws{i}", 4096, BF16) for i in range(3)]
    ST = sb("ST", DI)
    STb = sb("STb", DI, BF16)
    xt = [sb(f"xt{i}", D) for i in range(2)]
    htok = sb("htok", D, BF16)
    ubuf = [sb(f"ubuf{i}", NS * 94) for i in range(2)]
    xbuf = [sb(f"xbuf{i}", NS * 67) for i in range(2)]
    upg = [sb(f"upg{i}", NS * 66) for i in range(2)]
    upv = [sb(f"upv{i}", NS * 66) for i in range(2)]
    NTMP = 5
    tmp = [sb(f"tmp{i}", 512) for i in range(NTMP)]
    pft = sb("pft", NPF)
    cst = sb("cst", NCS)
    identb = sb("identb", 128, BF16)
    gam = sb("gam", D)
    hu = sb("hu", 16 * 30)
    hx = sb("hx", 48 * 3)
    hf = sb("hf", 88 * 2)
    dtT = sb("dtT", 2 * T)
    rstdb = sb("rstdb", T)
    lnm = sb("lnm", 3 * T)
    small = sb("small", 64)
    xtok = sb("xtok", DI, BF16)
    btok = sb("btok", 1024, BF16)
    dtk = sb("dtk", 5 * 64)
    Rg = [sb(f"Rg{i}", 512) for i in range(2)]
    EB = [sb(f"EB{i}", 512) for i in range(2)]
    Lm = [sb(f"Lm{i}", 512) for i in range(2)]
    MT_ = [sb(f"MTt{i}", 512, BF16) for i in range(2)]
    CE = [sb(f"CE{i}", 512, BF16) for i in range(2)]
    XW = [sb(f"XW{i}", 512, BF16) for i in range(2)]
    Ssb = [sb(f"Ssb{i}", 64) for i in range(2)]
    LAM = sb("LAM", 64)
    if with_pass1:
        lamall = sb("lamall", n_cores * 64)
        wj = sb("wj", n_cores * 64)
    PS = [es.enter_context(nc.psum_tensor(f"ps{i}", [128, 512], F32)) for i in range(8)]

    def pfc(name, c=0, n=1, rows=128):
        o, _ = PF[name]
        return pft[0:rows, o + c:o + c + n]

    def csv(name, rows=128, n=None):
        o, w = CS[name]
        return cst[0:rows, o:o + (n or w)]

    def V(t, f, p0, npart, off, dims):
        return bass.AP(t, p0 * f + off, [[f, npart]] + [list(d) for d in dims])

    def dma(q, out, in_, reads, writes, sem):
        S.op(q, lambda e, out=out, in_=in_: e.dma_start(out=out, in_=in_), reads=reads, writes=writes, dma=sem)

    def act(out, in_, func, reads, writes, bias=None, scale=None, accum=None):
        kw = {}
        if bias is not None:
            kw["bias"] = bias
        if scale is not None:
            kw["scale"] = scale
        if accum is not None:
            kw["accum_out"] = accum
        S.op("act", lambda e: e.activation(out=out, in_=in_, func=func, **kw), reads=reads, writes=writes)

    def tt(out, in0, in1, op, reads, writes, eng="dve"):
        S.op(eng, lambda e: e.tensor_tensor(out=out, in0=in0, in1=in1, op=op), reads=reads, writes=writes)

    def rsqrt(out, in_, scale, reads, writes):
        act(out, in_, AF.Sqrt, list(reads) + ["epsc"], writes, bias=small[0:out.shape[0], 40:41], scale=scale)
        S.op("dve", lambda e: e.reciprocal(out=out, in_=out), reads=writes, writes=writes)

    def ts(out, in0, s1, s2, op0, op1, reads, writes, eng="dve"):
        if s2 is None:
            S.op(eng, lambda e: e.tensor_scalar(out=out, in0=in0, scalar1=s1, scalar2=None, op0=op0),
                 reads=reads, writes=writes)
        else:
            S.op(eng, lambda e: e.tensor_scalar(out=out, in0=in0, scalar1=s1, scalar2=s2, op0=op0, op1=op1),
                 reads=reads, writes=writes)

    def stt(out, in0, sc, in1, op0, op1, reads, writes, eng="dve"):
        S.op(eng, lambda e: e.scalar_tensor_tensor(out=out, in0=in0, scalar=sc, in1=in1, op0=op0, op1=op1),
             reads=reads, writes=writes)

    def mm(out, lhsT, rhs, start, stop, reads, writes, last):
        S.op("pe", lambda e: e.matmul(out, lhsT, rhs, start=start, stop=stop), reads=reads, writes=writes, inc=last)

    def tr(out, in_, ident, reads, writes, last):
        S.op("pe", lambda e: e.transpose(out, in_, ident), reads=reads, writes=writes, inc=last)

    def psum():
        i = S.rot("ps", 7)
        return PS[i], ("ps", i)

    def tmpb():
        i = S.rot("tmp", NTMP)
        return tmp[i], ("tmp", i)

    def wload(wd, tile, F, ncols=128):
        i = S.rot("ws", 3)
        slot = WS[i]
        dma("pool", slot[:, 0:F], wd[tile], [], [("ws", i)], f"w{i}")
        return slot, ("ws", i), ncols

    def wl(slotinfo, kc, c0, n):
        slot, _, ncols = slotinfo
        return slot[:, kc * ncols + c0: kc * ncols + c0 + n]

    RBk = lambda page: ("RB", page)
    xsT = lambda c: RB[:, c * T:(c + 1) * T]
    BCT = lambda c: RB[:, (32 + c) * T:(33 + c) * T]
    ucv = lambda c: RB[:, 2 * c * T:(2 * c + 2) * T].bitcast(F32)
    uav = lambda c: RB[:, (32 + c) * T:(33 + c) * T]
    gTv = lambda j: RB[:, j * T:(j + 1) * T]
    hTv = lambda kc: hT[:, kc * T:(kc + 1) * T]
    mrgv = lambda c: mrg[:, c * T:(c + 1) * T]

    dma("sp", pft[:, :], pf_d, [], ["pft"], "c0")
    dma("sp", cst[:, :], cs_d, [], ["cst"], "c1")
    act(identb[:, :], csv("ident"), AF.Copy, ["cst"], ["identb"])
    act(small[0:64, 0:1], pfc("alog", rows=64), AF.Exp, ["pft"], ["small"])
    ts(small[0:64, 0:1], small[0:64, 0:1], -1.0, None, ALU.mult, None, ["small"], ["small"])
    aneg = small[0:64, 0:1]
    S.op("dve", lambda e: e.memset(small[:, 40:41], EPS), reads=[], writes=["epsc"])
    S.op("dve", lambda e: e.memset(small[:, 41:42], 1.0), reads=["epsc"], writes=["epsc"])
    identf = csv("ident")
    Uf = csv("U", rows=64)
    negm = csv("negm", rows=64)
    onesf = csv("ones", rows=64)
    ones128 = csv("ones")

    def proj(wd, tile0, ntiles, rhs_fn, nkc, evac, rhs_reads, M=128):
        for b in range(ntiles):
            wsl = wload(wd, tile0 + b, nkc * 128)
            ps, psk = psum()
            for kc in range(nkc):
                mm(ps[0:M, 0:T], wl(wsl, kc, 0, M), rhs_fn(kc), kc == 0, kc == nkc - 1,
                   [wsl[1]] + rhs_reads(kc), [psk], kc == nkc - 1)
            evac(b, ps, psk)

    def hist_ops(buf, bk, W1, width, c, nch, slots, hcarry, hck, hin_d, osb, osk, odr, save_carry, mask_carry):
        SW = W1 + 64
        merged_prev = all(s_["hist"] == "prev" for s_ in slots[1:])
        if merged_prev:
            act(V(buf, NS * SW, 0, 128, SW, [[SW, NS - 1], [1, W1]]), V(buf, NS * SW, 0, 128, 64, [[SW, NS - 1], [1, W1]]),
                AF.Copy, [bk], [bk])
        for j, sl in enumerate(slots):
            dst = buf[:, j * SW: j * SW + W1]
            src = sl["hist"]
            if src == "prev" and merged_prev:
                continue
            if src == "prev":
                act(dst, buf[:, (j - 1) * SW + 64:(j - 1) * SW + 64 + W1], AF.Copy, [bk], [bk])
            elif src == "carry":
                act(dst, hcarry[:, c * W1:(c + 1) * W1], AF.Copy, [hck], [bk])
            elif src[0] == "cache":
                dma("sp", dst, hin_d[src[1], :, c * W1:(c + 1) * W1], [], [bk], f"hin_{bk[0]}{bk[1]}_{j}")
            else:
                S.op("dve", lambda e, dst=dst: e.memset(dst, 0.0), reads=[], writes=[bk])
        for j, sl in enumerate(slots):
            if sl.get("out") is not None:
                tail = buf[:, j * SW + 64: j * SW + 64 + W1]
                dma("sp", odr[sl["out"], :, c * W1:(c + 1) * W1], tail, [bk], [], f"hout_{bk[0]}{bk[1]}")
        if save_carry:
            tail = buf[:, (NS - 1) * SW + 64:(NS - 1) * SW + 64 + W1]
            if mask_carry:
                ts(hcarry[:, c * W1:(c + 1) * W1], tail, pfc("mk"), None, ALU.mult, None, [bk, "pft"], [hck])
            else:
                act(hcarry[:, c * W1:(c + 1) * W1], tail, AF.Copy, [bk], [hck])

    def conv(buf, bk, W1, wname, bname, c, out_ap, outk, nacc):
        SW = W1 + 64
        ntap = W1 + 1
        wo = PF[wname][0]
        o3 = lambda ap: ap.rearrange("p (s t) -> p s t", s=NS)
        accs = []
        per = (ntap + nacc - 1) // nacc
        for a in range(nacc):
            if a == nacc - 1:
                acc, acck = out_ap, outk
            else:
                tb, tk = tmpb()
                acc, acck = tb[:, 0:T], [tk]
            taps = list(range(a * per, min(ntap, (a + 1) * per)))
            for n, w in enumerate(taps):
                src = V(buf, NS * SW, 0, 128, w, [[SW, NS], [1, 64]])
                wcol = pft[:, wo + c * ntap + w: wo + c * ntap + w + 1]
                if n == 0:
                    if a == 0:
                        ts(o3(acc), src, wcol, pfc(bname, c), ALU.mult, ALU.add, [bk, "pft"], acck)
                    else:
                        ts(o3(acc), src, wcol, None, ALU.mult, None, [bk, "pft"], acck)
                else:
                    stt(o3(acc), src, wcol, o3(acc), ALU.mult, ALU.add, [bk, "pft"] + acck, acck)
            accs.append((acc, acck))
        for a in range(nacc - 1):
            tt(out_ap, out_ap, accs[a][0], ALU.add, outk + accs[a][1], outk)

    def norm_to_T(src_rows_fn, dstT, dstk, gname, src_is_dram):
        for s in range(NSUB):
            if src_is_dram:
                xi = S.rot("xt", 2)
                xb, xk = xt[xi], ("xt", xi)
                dma("sp", xb[:, :], src_rows_fn(s), [], [xk], f"xin{xi}")
                src, srck = xb[:, :], [xk]
            else:
                src, srck = src_rows_fn(s), ["xmid"]
            S.op("dve", lambda e: e.memset(small[:, 8:12], 0.0), reads=[], writes=["ss"])
            for n in range(4):
                act(sb_junk[:, 0:512], src[:, n * 512:(n + 1) * 512], AF.Square, srck, ["junk", "ss"],
                    accum=small[:, 8 + n:9 + n])
            ts(small[:, 12:13], small[:, 8:9], small[:, 9:10], None, ALU.add, None, ["ss"], ["ss2"])
            ts(small[:, 12:13], small[:, 12:13], small[:, 10:11], None, ALU.add, None, ["ss", "ss2"], ["ss2"])
            ts(small[:, 12:13], small[:, 12:13], small[:, 11:12], None, ALU.add, None, ["ss", "ss2"], ["ss2"])
            rsqrt(small[:, 12:13], small[:, 12:13], 1.0 / D, ["ss2"], ["ss2"])
            ts(htok[:, :], src, small[:, 12:13], None, ALU.mult, None, srck + ["ss2"], ["htok"])
            for c4 in range(4):
                ps, psk = psum()
                pb = ps[:].bitcast(BF16)
                for q in range(4):
                    c = c4 * 4 + q
                    tr(pb[:, q * 128:(q + 1) * 128], htok[:, c * 128:(c + 1) * 128], identb[:, :],
                       ["htok", "identb"], [psk], q == 3)
                for q in range(4):
                    c = c4 * 4 + q
                    act(dstT[:, c * T + s * 128: c * T + (s + 1) * 128], pb[:, q * 128:(q + 1) * 128], AF.Copy,
                        [psk, "pft"], [(dstk, c)], scale=pfc(gname, c))

    sb_junk = sb("junk", 512)

    def ssd_prep_T(j, masked):
        t0 = j * L
        for b in range(4):
            ps, psk = psum()
            pb = ps[:].bitcast(BF16)
            for q in range(8):
                c = b * 8 + q
                tr(pb[0:64, q * 128:(q + 1) * 128], xsT(c)[:, t0:t0 + L], identb[:, :], [RBk(c), "identb"], [psk], q == 7)
            act(xtok[0:64, b * 1024:(b + 1) * 1024], pb[0:64, 0:1024], AF.Copy, [psk], [("xtok", b)])
        ps, psk = psum()
        pb = ps[:].bitcast(BF16)
        for g in range(8):
            tr(pb[0:64, g * 128:(g + 1) * 128], BCT(g)[:, t0:t0 + L], identb[:, :], [RBk(32 + g), "identb"], [psk], g == 7)
        act(btok[0:64, :], pb[0:64, 0:1024], AF.Copy, [psk], ["btok"])
        ps, psk = psum()
        tr(ps[0:64, 0:64], dtT[0:64, t0:t0 + L], identf[0:64, 0:64], ["dtT", "cst"], [psk], False)
        tr(ps[0:64, 64:128], dtT[0:64, T + t0:T + t0 + L], identf[0:64, 0:64], ["dtT", "cst"], [psk], True)
        if masked:
            ts(dtk[0:64, 0:128], ps[0:64, 0:128], pfc("mk", rows=64), None, ALU.mult, None, [psk, "pft"], ["dtk"])
        else:
            act(dtk[0:64, 0:128], ps[0:64, 0:128], AF.Copy, [psk], ["dtk"])
        ps, psk = psum()
        mm(ps[0:64, 0:64], Uf[:, 0:64], dtk[0:64, 64:128], True, True, ["cst", "dtk"], [psk], False)
        mm(ps[0:64, 64:128], onesf[:, 0:64], dtk[0:64, 64:128], True, True, ["cst", "dtk"], [psk], True)
        act(dtk[0:64, 128:192], ps[0:64, 0:64], AF.Copy, [psk], ["dtk2"])
        tt(dtk[0:64, 192:256], ps[0:64, 64:128], dtk[0:64, 128:192], ALU.subtract, [psk, "dtk2"], ["dtk3"])
        act(dtk[0:64, 192:256], dtk[0:64, 192:256], AF.Exp, ["dtk3"], ["dtk3"])
        tt(dtk[0:64, 256:320], dtk[0:64, 192:256], dtk[0:64, 0:64], ALU.mult, ["dtk3", "dtk"], ["dtk4"])

    def ssd_slot(j, masked, want_y, acc_lam):
        t0 = j * L
        ssd_prep_T(j, masked)
        if STOP <= 4.2:
            return
        for g in range(8):
            i2 = S.rot("grp", 2)
            R, EBg, Lmg, MTg, CEg, XWg, Sg = Rg[i2], EB[i2], Lm[i2], MT_[i2], CE[i2], XW[i2], Ssb[i2]
            k = lambda n: (n, i2)
            dta_b = V(dtk, 320, 0, 64, 64 + 8 * g, [[1, 8], [0, 64]])
            U_b = V(cst, NCS, 0, 64, CS["U"][0], [[0, 8], [1, 64]])
            tt(V(R, 512, 0, 64, 0, [[64, 8], [1, 64]]), dta_b, U_b, ALU.mult, ["dtk", "cst"], [k("R")])
            psc, psck = psum()
            mm(psc[:, 0:512], onesf[:, 0:128], R[0:64, 0:512], True, True, ["cst", k("R")], [psck], True)
            if acc_lam:
                lastv = V(psc, 512, 0, 128, 63, [[64, 8]])
                tt(LAM[:, 8 * g:8 * g + 8], LAM[:, 8 * g:8 * g + 8], lastv, ALU.add, [psck, "LAM"], ["LAM"])
            act(EBg[:, 0:512], psc[:, 0:512], AF.Exp, [psck], [k("EB")])
            if STOP <= 4.4:
                continue
            if want_y and not SKIPY:
                cum_b = V(dtk, 320, 0, 64, 128 + 8 * g, [[1, 8], [0, 64]])
                tb, tk = tmpb()
                seg = V(tb, 512, 0, 64, 0, [[64, 8], [1, 64]])
                tt(seg, V(psc, 512, 0, 64, 0, [[64, 8], [1, 64]]), cum_b, ALU.subtract, [psck, "dtk2"], [tk])
                negm_b = V(cst, NCS, 0, 64, CS["negm"][0], [[0, 8], [1, 64]])
                tt(seg, seg, negm_b, ALU.add, [tk, "cst"], [tk])
                act(Lmg[0:64, 0:512], tb[0:64, 0:512], AF.Exp, [tk], [k("Lm")])
                if STOP <= 4.5:
                    continue
                pss, pssk = psum()
                mm(pss[0:64, 0:64], BCT(g)[:, t0:t0 + L], BCT(8 + g)[:, t0:t0 + L], True, True,
                   [RBk(32 + g), RBk(40 + g)], [pssk], True)
                act(Sg[0:64, 0:64], pss[0:64, 0:64], AF.Copy, [pssk], [k("S")])
                dt_b = V(dtk, 320, 0, 64, 8 * g, [[1, 8], [0, 64]])
                Lm3 = V(Lmg, 512, 0, 64, 0, [[64, 8], [1, 64]])
                tt(Lm3, Lm3, dt_b, ALU.mult, [k("Lm"), "dtk"], [k("Lm")])
                S_b = V(Sg, 64, 0, 64, 0, [[0, 8], [1, 64]])
                tt(V(MTg, 512, 0, 64, 0, [[64, 8], [1, 64]]), Lm3, S_b, ALU.mult, [k("Lm"), k("S")], [k("MT")])
                C_b = V(RB, 48 * T, 0, 128, (40 + g) * T + t0, [[0, 8], [1, 64]])
                tt(V(CEg, 512, 0, 128, 0, [[64, 8], [1, 64]]), V(EBg, 512, 0, 128, 0, [[64, 8], [1, 64]]), C_b,
                   ALU.mult, [k("EB"), RBk(40 + g)], [k("CE")])
                if STOP <= 4.6:
                    continue
                psy, psyk = psum()
                for q in range(4):
                    cidx = 4 * g + q
                    for hh in range(2):
                        o = psy[:, (2 * q + hh) * 64:(2 * q + hh + 1) * 64]
                        mm(o, xtok[0:64, cidx * 128:(cidx + 1) * 128], MTg[0:64, (2 * q + hh) * 64:(2 * q + hh + 1) * 64],
                           True, False, [("xtok", cidx // 8), k("MT")], [psyk], False)
                        mm(o, STb[:, cidx * 128:(cidx + 1) * 128], CEg[:, (2 * q + hh) * 64:(2 * q + hh + 1) * 64],
                           False, True, [("STb", g), k("CE")], [psyk], q == 3 and hh == 1)
                if STOP <= 4.7:
                    continue
                for hh in range(2):
                    p0 = 64 * hh
                    xs3 = V(RB, 48 * T, p0, 64, 4 * g * T + t0, [[T, 4], [1, 64]])
                    d_b = V(pft, NPF, p0, 64, PF["dsk"][0] + 4 * g, [[1, 4], [0, 64]])
                    y_ps = V(psy, 512, p0, 64, hh * 64, [[128, 4], [1, 64]])
                    rk = [RBk(4 * g + q) for q in range(4)]
                    tt(xs3, xs3, d_b, ALU.mult, rk + ["pft"], rk)
                    tt(xs3, xs3, y_ps, ALU.add, rk + [psyk], rk)
                if STOP <= 4.8:
                    continue
            w_b = V(dtk, 320, 0, 64, 256 + 8 * g, [[1, 8], [0, 64]])
            tt(V(XWg, 512, 0, 64, 0, [[64, 8], [1, 64]]), V(xtok, DI, 0, 64, 512 * g, [[64, 8], [1, 64]]), w_b,
               ALU.mult, [("xtok", g // 2), "dtk4"], [k("XW")])
            pst, pstk = psum()
            mm(pst[:, 0:512], btok[0:64, g * 128:(g + 1) * 128], XWg[0:64, 0:512], True, True, ["btok", k("XW")], [pstk], True)
            st3 = V(ST, DI, 0, 128, 512 * g, [[64, 8], [1, 64]])
            eb_last = V(EBg, 512, 0, 128, 63, [[64, 8], [0, 64]])
            tt(st3, st3, eb_last, ALU.mult, [("ST", g), k("EB")], [("ST", g)])
            tt(ST[:, 512 * g:512 * g + 512], ST[:, 512 * g:512 * g + 512], pst[:, 0:512], ALU.add, [("ST", g), pstk], [("ST", g)])
            act(STb[:, 512 * g:512 * g + 512], ST[:, 512 * g:512 * g + 512], AF.Copy, [("ST", g)], [("STb", g)])

    def load_state(src_ap, reads=()):
        for g in range(8):
            dma("sp", ST[:, 512 * g:512 * g + 512], src_ap[:, 512 * g:512 * g + 512], list(reads), [("ST", g)], f"stin{g}")
            act(STb[:, 512 * g:512 * g + 512], ST[:, 512 * g:512 * g + 512], AF.Copy, [("ST", g)], [("STb", g)])

    def zero_state():
        for g in range(8):
            S.op("dve", lambda e, g=g: e.memset(ST[:, 512 * g:512 * g + 512], 0.0), reads=[], writes=[("ST", g)])
            act(STb[:, 512 * g:512 * g + 512], ST[:, 512 * g:512 * g + 512], AF.Copy, [("ST", g)], [("STb", g)])

    def store_state(dst_ap):
        for g in range(8):
            dma("sp", dst_ap[:, 512 * g:512 * g + 512], ST[:, 512 * g:512 * g + 512], [("ST", g)], [], f"stout{g}")

    def branchB_front(slots, save_carry, use_hist_outs):
        def evac_x(c, ps, psk):
            bi = S.rot("xbuf", 2)
            buf, bk = xbuf[bi], ("xbuf", bi)
            act(V(buf, NS * 67, 0, 128, 3, [[67, NS], [1, 64]]), ps[:, 0:T].rearrange("p (s t) -> p s t", s=NS),
                AF.Copy, [psk], [bk])
            sl2 = slots if use_hist_outs else [dict(s_, out=None) for s_ in slots]
            hist_ops(buf, bk, 3, 4, c, 48, sl2, hx, "hx", xh_d, None, None, oxh, save_carry, False)
            tb, tk = tmpb()
            conv(buf, bk, 3, "cbw", "cbb", c, tb[:, 0:T], [tk], 1)
            dst = xsT(c) if c < 32 else BCT(c - 32)
            act(dst, tb[:, 0:T], AF.Silu, [tk], [RBk(c)])
        proj(w_in_d, 64, 48, hTv, KC, evac_x, lambda kc: [("hT", kc)])

        def evac_dt(c, ps, psk):
            act(dtT[0:64, 0:T], ps[0:64, 0:T], AF.Exp, [psk, "pft"], ["dtT"], bias=pfc("dtb", rows=64))
            act(dtT[0:64, 0:T], dtT[0:64, 0:T], AF.Ln, ["dtT", "epsc"], ["dtT"], bias=small[0:64, 41:42])
            ts(dtT[0:64, T:2 * T], dtT[0:64, 0:T], aneg, None, ALU.mult, None, ["dtT", "small"], ["dtT"])
        proj(w_in_d, 112, 1, hTv, KC, evac_dt, lambda kc: [("hT", kc)], M=64)

    def layer_mt(row0, slots, save_carry, mask_up_carry):
        xrows = lambda s: xin[row0 + s * 128: row0 + (s + 1) * 128, :]
        norm_to_T(xrows, hT, "hT", "npre", True)
        if STOP <= 1:
            return
        def evac_a_pair():
            pass
        for c in range(16):
            wa = wload(w_in_d, c, KC * 128)
            wg = wload(w_in_d, 16 + c, KC * 128)
            psa, psak = psum()
            psg, psgk = psum()
            for kc in range(KC):
                mm(psa[:, 0:T], wl(wa, kc, 0, 128), hTv(kc), kc == 0, kc == KC - 1, [wa[1], ("hT", kc)], [psak], kc == KC - 1)
            for kc in range(KC):
                mm(psg[:, 0:T], wl(wg, kc, 0, 128), hTv(kc), kc == 0, kc == KC - 1, [wg[1], ("hT", kc)], [psgk], kc == KC - 1)
            tb, tk = tmpb()
            act(tb[:, 0:T], psg[:, 0:T], AF.Sigmoid, [psgk], [tk])
            bi = S.rot("ubuf", 2)
            buf, bk = ubuf[bi], ("ubuf", bi)
            tt(V(buf, NS * 94, 0, 128, 30, [[94, NS], [1, 64]]), psa[:, 0:T].rearrange("p (s t) -> p s t", s=NS),
               tb[:, 0:T].rearrange("p (s t) -> p s t", s=NS), ALU.mult, [psak, tk], [bk])
            hist_ops(buf, bk, 30, 31, c, 16, slots, hu, "hu", uh_d, None, None, ouh, save_carry, False)
            conv(buf, bk, 30, "caw", "cab", c, ucv(c), [RBk(2 * c), RBk(2 * c + 1)], 2)
        if STOP <= 2:
            return
        psm, psmk = psum()
        pss, pssk = psum()
        for c in range(16):
            mm(psm[:, 0:T], ones128, ucv(c), c == 0, c == 15, ["cst", RBk(2 * c), RBk(2 * c + 1)], [psmk], c == 15)
        for c in range(16):
            tb, tk = tmpb()
            act(tb[:, 0:T], ucv(c), AF.Square, [RBk(2 * c), RBk(2 * c + 1)], [tk])
            mm(pss[:, 0:T], ones128, tb[:, 0:T], c == 0, c == 15, ["cst", tk], [pssk], True)
        if STOP <= 2.3:
            return
        mean, rstd, nmr = lnm[:, 0:T], lnm[:, T:2 * T], lnm[:, 2 * T:3 * T]
        act(mean, psm[:, 0:T], AF.Copy, [psmk], ["lnm0"], scale=1.0 / D)
        tt(rstd, mean, mean, ALU.mult, ["lnm0"], ["lnm1"])
        stt(rstd, pss[:, 0:T], 1.0 / D, rstd, ALU.mult, ALU.subtract, [pssk, "lnm1"], ["lnm1"])
        rsqrt(rstd, rstd, 1.0, ["lnm1"], ["lnm1"])
        stt(nmr, mean, -1.0, rstd, ALU.mult, ALU.mult, ["lnm0", "lnm1"], ["lnm2"])
        for c in range(16):
            tb, tk = tmpb()
            tt(tb[:, 0:T], ucv(c), rstd, ALU.mult, [RBk(2 * c), RBk(2 * c + 1), "lnm1"], [tk])
            tt(tb[:, 0:T], tb[:, 0:T], nmr, ALU.add, [tk, "lnm2"], [tk])
            act(uav(c), tb[:, 0:T], AF.Silu, [tk, "pft"], [RBk(32 + c)], bias=pfc("lnb", c), scale=pfc("lng", c))
        if STOP <= 2.6:
            return
        for c in range(16):
            wy = wload(w_a_d, c, KC * 128)
            wg = wload(w_in_d, 113 + c, KC * 128)
            psy, psyk = psum()
            psg, psgk = psum()
            for kc in range(KC):
                mm(psy[:, 0:T], wl(wy, kc, 0, 128), uav(kc), kc == 0, kc == KC - 1, [wy[1], RBk(32 + kc)], [psyk], kc == KC - 1)
            for kc in range(KC):
                mm(psg[:, 0:T], wl(wg, kc, 0, 128), hTv(kc), kc == 0, kc == KC - 1, [wg[1], ("hT", kc)], [psgk], kc == KC - 1)
            tb, tk = tmpb()
            act(tb[:, 0:T], psg[:, 0:T], AF.Sigmoid, [psgk, "pft"], [tk], bias=pfc("bg", c))
            tt(mrgv(c), psy[:, 0:T], tb[:, 0:T], ALU.mult, [psyk, tk], [("mrg", c)])
        if STOP <= 3:
            return
        branchB_front(slots, save_carry, True)
        if STOP <= 4:
            return
        for j, sl in enumerate(slots):
            st = sl.get("state")
            if st is None:
                continue
            if st[0] == "load":
                load_state(st_d[st[1]])
            elif st[0] == "zero":
                zero_state()
            elif st[0] == "halo":
                halo_state()
            ssd_slot(j, sl.get("masked", False), True, False)
            if sl.get("out") is not None:
                store_state(ost[sl["out"]])
        if STOP <= 5:
            return
        psq, psqk = PS[7], ("ps", 7)
        for c0 in range(0, 32):
            wz = wload(w_in_d, 32 + c0, KC * 128)
            for bi in range(1):
                c = c0 + bi
                ps, psk = psum()
                for kc in range(KC):
                    mm(ps[:, 0:T], wl(wz, kc, bi * 128, 128), hTv(kc), kc == 0, kc == KC - 1, [wz[1], ("hT", kc)], [psk], kc == KC - 1)
                tb, tk = tmpb()
                act(tb[:, 0:T], ps[:, 0:T], AF.Silu, [psk], [tk])
                tt(tb[:, 0:T], tb[:, 0:T], xsT(c), ALU.mult, [tk, RBk(c)], [tk])
                tb2, tk2 = tmpb()
                act(tb2[:, 0:T], tb[:, 0:T], AF.Square, [tk], [tk2])
                mm(psq[:, 0:T], ones128, tb2[:, 0:T], c == 0, c == 31, ["cst", tk2], [psqk], True)
                ts(xsT(c), tb[:, 0:T], pfc("sng", c), None, ALU.mult, None, [tk, "pft"], [RBk(c)])
        rsqrt(rstdb[:, 0:T], psq[:, 0:T], 1.0 / DI, [psqk], ["rstdb"])
        for c in range(16):
            wy = wload(w_b_d, c, 32 * 128)
            wg = wload(w_in_d, 129 + c, KC * 128)
            psy, psyk = psum()
            psg, psgk = psum()
            for kc in range(32):
                mm(psy[:, 0:T], wl(wy, kc, 0, 128), xsT(kc), kc == 0, kc == 31, [wy[1], RBk(kc)], [psyk], kc == 31)
            for kc in range(KC):
                mm(psg[:, 0:T], wl(wg, kc, 0, 128), hTv(kc), kc == 0, kc == KC - 1, [wg[1], ("hT", kc)], [psgk], kc == KC - 1)
            tb, tk = tmpb()
            act(tb[:, 0:T], psg[:, 0:T], AF.Sigmoid, [psgk, "pft"], [tk], bias=pfc("bg", 16 + c))
            tb2, tk2 = tmpb()
            tt(tb2[:, 0:T], psy[:, 0:T], rstdb[:, 0:T], ALU.mult, [psyk, "rstdb"], [tk2])
            tt(tb2[:, 0:T], tb2[:, 0:T], tb[:, 0:T], ALU.mult, [tk2, tk], [tk2])
            tt(mrgv(c), mrgv(c), tb2[:, 0:T], ALU.add, [("mrg", c), tk2], [("mrg", c)])
        if STOP <= 6:
            return
        dma("sp", gam[:, :], gam1_d, [], ["gam"], "gam")
        tokmajor_out(w_o_d, KC, 4, lambda kc, s: mrg[:, kc * T + s * 128: kc * T + (s + 1) * 128],
                     lambda kc: [("mrg", kc)], lambda s: xmid[:, s * D:(s + 1) * D], ["xmid"], 10)
        for s in range(NSUB):
            xi = S.rot("xt", 2)
            xb, xk = xt[xi], ("xt", xi)
            dma("sp", xb[:, :], xrows(s), [], [xk], f"xin{xi}")
            finish_tok(xmid[:, s * D:(s + 1) * D], ["xmid"], 10 + 4 * s, xb[:, :], [xk])
        if STOP <= 7:
            return
        norm_to_T(lambda s: xmid[:, s * D:(s + 1) * D], hT, "hT", "nfpre", False)
        for j in range(NJ):
            wg = wload(w_up_d, j, KC * 128)
            wv = wload(w_up_d, NJ + j, KC * 128)
            psg, psgk = psum()
            psv, psvk = psum()
            for kc in range(KC):
                mm(psg[:, 0:T], wl(wg, kc, 0, 128), hTv(kc), kc == 0, kc == KC - 1, [wg[1], ("hT", kc)], [psgk], kc == KC - 1)
            for kc in range(KC):
                mm(psv[:, 0:T], wl(wv, kc, 0, 128), hTv(kc), kc == 0, kc == KC - 1, [wv[1], ("hT", kc)], [psvk], kc == KC - 1)
            res = []
            for (ps, psk, bufs, nm, cc) in ((psg, psgk, upg, "upg", j), (psv, psvk, upv, "upv", NJ + j)):
                bi = S.rot(nm, 2)
                buf, bk = bufs[bi], (nm, bi)
                act(V(buf, NS * 66, 0, 128, 2, [[66, NS], [1, 64]]), ps[:, 0:T].rearrange("p (s t) -> p s t", s=NS),
                    AF.Copy, [psk], [bk])
                hist_ops(buf, bk, 2, 3, cc, 88, slots, hf, "hf", fh_d, None, None, ofh, save_carry, mask_up_carry)
                tb, tk = tmpb()
                conv(buf, bk, 2, "fcw", "fcb", cc, tb[:, 0:T], [tk], 1)
                res.append((tb, tk))
            (tg, tgk), (tv, tvk) = res
            act(tg[:, 0:T], tg[:, 0:T], AF.Gelu_apprx_tanh, [tgk], [tgk])
            tt(gTv(j), tg[:, 0:T], tv[:, 0:T], ALU.mult, [tgk, tvk], [RBk(j)])
        if STOP <= 8:
            return
        dma("sp", gam[:, :], gam2_d, [], ["gam"], "gam")
        ybufs = []
        for s in range(NSUB):
            xi = S.rot("xt", 2)
            ybufs.append((xt[xi], ("xt", xi)))
        tokmajor_out(w_dn_d, NJ, 4, lambda kc, s: RB[:, kc * T + s * 128: kc * T + (s + 1) * 128],
                     lambda kc: [RBk(kc)], lambda s: ybufs[s][0][:, :], None, 20, outks=[[b[1]] for b in ybufs])
        for s in range(NSUB):
            yb, yk = ybufs[s]
            finish_tok(yb[:, :], [yk], 20 + 4 * s, xmid[:, s * D:(s + 1) * D], ["xmid"])
            dma("sp", yout[row0 + s * 128: row0 + (s + 1) * 128, :], yb[:, :], [yk], [], f"yout{yk[1]}")

    def tokmajor_out(wd, nkc, piece, lhs_fn, lhs_reads, dst_fn, dstk, sscol, outks=None):
        npiece = nkc // piece
        for s in range(NSUB):
            S.op("dve", lambda e, s=s: e.memset(small[:, sscol + 4 * s: sscol + 4 * s + 4], 0.0), reads=[],
                 writes=[("ssq", sscol, s)])
        for n in range(4):
            banks = [psum() for _ in range(NSUB)]
            for pc in range(npiece):
                wsl = wload(wd, n * npiece + pc, piece * 512, 512)
                for s in range(NSUB):
                    ps, psk = banks[s]
                    for kk in range(piece):
                        kc = pc * piece + kk
                        mm(ps[:, 0:512], lhs_fn(kc, s), wl(wsl, kk, 0, 512), kc == 0, kc == nkc - 1,
                           [wsl[1]] + lhs_reads(kc), [psk], kk == piece - 1)
            for s in range(NSUB):
                ps, psk = banks[s]
                dk = dstk if outks is None else outks[s]
                act(sb_junk[:, 0:512], ps[:, 0:512], AF.Square, [psk], ["junk", ("ssq", sscol, s)],
                    accum=small[:, sscol + 4 * s + n: sscol + 4 * s + n + 1])
                S.op("dve", lambda e, s=s, ps=ps, n=n: e.tensor_copy(out=dst_fn(s)[:, n * 512:(n + 1) * 512], in_=ps[:, 0:512]),
                     reads=[psk], writes=dk)

    def finish_tok(buf, bufk, sscol, resid, residk):
        base = 10 if sscol < 20 else 20
        k = ("ssq", base, (sscol - base) // 4)
        ts(small[:, 30:31], small[:, sscol:sscol + 1], small[:, sscol + 1:sscol + 2], None, ALU.add, None, [k], ["fin"])
        ts(small[:, 30:31], small[:, 30:31], small[:, sscol + 2:sscol + 3], None, ALU.add, None, ["fin", k], ["fin"])
        ts(small[:, 30:31], small[:, 30:31], small[:, sscol + 3:sscol + 4], None, ALU.add, None, ["fin", k], ["fin"])
        rsqrt(small[:, 30:31], small[:, 30:31], 1.0 / D, ["fin"], ["fin"])
        stt(buf, buf, small[:, 30:31], gam[:, :], ALU.mult, ALU.mult, bufk + ["fin", "gam"], bufk)
        tt(buf, buf, resid, ALU.add, bufk + residk, bufk)

    if with_pass1:
        S.op("dve", lambda e: e.memset(LAM[:, :], 0.0), reads=[], writes=["LAM"])
        zero_state()
        n1 = n_own_mt + 1
        for m in range(n1):
            row0 = m * T
            xrows = lambda s, row0=row0: x1in[row0 + s * 128: row0 + (s + 1) * 128, :]
            norm_to_T(xrows, hT, "hT", "npre", True)
            if m == 0:
                slots1 = [dict(hist="zero"), dict(hist="prev", chain=True, masked=True), dict(hist="prev", chain=True),
                          dict(hist="prev", chain=True)]
            elif m < n1 - 1:
                slots1 = [dict(hist="carry", chain=True)] + [dict(hist="prev", chain=True)] * 3
            else:
                slots1 = [dict(hist="zero"), dict(hist="prev"), dict(hist="prev", chain=True), dict(hist="prev")]
            branchB_front(slots1, m < n1 - 2, False)
            for j, sl in enumerate(slots1):
                if sl.get("chain"):
                    ssd_slot(j, sl.get("masked", False), False, True)
        for g in range(8):
            dma("sp", cc_in.ap()[:, 512 * g:512 * g + 512], ST[:, 512 * g:512 * g + 512], [("ST", g)], ["ccin"], "ccp")
        dma("sp", cc_in.ap()[:, DI:DI + NH], LAM[:, :], ["LAM"], ["ccin"], "ccp")
        S.op("pool", lambda e: e.collective_compute("AllGather", ALU.bypass, replica_groups=[list(range(n_cores))],
                                                     ins=[cc_in.ap().opt()], outs=[cc_out.ap().opt()]),
             reads=["ccin"], writes=["ccout"])
        cco = cc_out.ap()
        dma("sp", V(lamall, n_cores * 64, 0, 128, 0, [[64, n_cores], [1, 64]]),
            cco[:, DI:DI + NH].rearrange("(r p) c -> p r c", p=128), ["ccout"], ["lamall"], "cc2")
        for jj in range(n_cores):
            wjj = wj[:, jj * 64:(jj + 1) * 64]
            for m_ in range(n_cores):
                selc = pfc("sel", jj * 8 + m_)
                if m_ == 0:
                    ts(wjj, lamall[:, 0:64], selc, None, ALU.mult, None, ["lamall", "pft"], [("wj", jj)])
                else:
                    stt(wjj, lamall[:, m_ * 64:(m_ + 1) * 64], selc, wjj, ALU.mult, ALU.add, ["lamall", "pft", ("wj", jj)], [("wj", jj)])
            act(wjj, wjj, AF.Exp, [("wj", jj)], [("wj", jj)])
            ts(wjj, wjj, pfc("selm", jj), None, ALU.mult, None, [("wj", jj), "pft"], [("wj", jj)])
        for g in range(8):
            S.op("dve", lambda e, g=g: e.memset(ST[:, 512 * g:512 * g + 512], 0.0), reads=[], writes=[("ST", g)])
        Pj = [xmid[:, 0:DI], RB[:, 0:2 * DI].bitcast(F32)]
        Pjk = [["xmid"], [RBk(p) for p in range(32)]]
        for jj in range(n_cores):
            pi = S.rot("Pj", 2)
            pb, pk = Pj[pi], Pjk[pi]
            dma("sp", pb, cco[jj * 128:(jj + 1) * 128, 0:DI], ["ccout"], pk, f"cc2_{pi}")
            for g in range(8):
                w_b = V(wj, n_cores * 64, 0, 128, jj * 64 + 8 * g, [[1, 8], [0, 64]])
                p3 = pb[:, 512 * g:512 * g + 512].rearrange("p (h q) -> p h q", h=8)
                tt(p3, p3, w_b, ALU.mult, pk + [("wj", jj)], pk)
                tt(ST[:, 512 * g:512 * g + 512], ST[:, 512 * g:512 * g + 512], pb[:, 512 * g:512 * g + 512], ALU.add,
                   [("ST", g)] + pk, [("ST", g)])
        SIN = xmid[:, 0:DI]
        for g in range(8):
            act(SIN[:, 512 * g:512 * g + 512], ST[:, 512 * g:512 * g + 512], AF.Copy, [("ST", g), "xmid"], ["xmid"])

    mt0 = [dict(hist=("cache", 0), state=("load", 0), out=0),
           dict(hist=("cache", 1), state=("load", 1), out=1),
           dict(hist="zero"),
           dict(hist="prev", state=("halo",), masked=True)]
    def halo_state():
        if with_pass1:
            for g in range(8):
                act(ST[:, 512 * g:512 * g + 512], SIN[:, 512 * g:512 * g + 512], AF.Copy, ["xmid"], [("ST", g)])
                act(STb[:, 512 * g:512 * g + 512], ST[:, 512 * g:512 * g + 512], AF.Copy, [("ST", g)], [("STb", g)])
        else:
            zero_state()
    layer_mt(0, mt0, True, True)
    for m in range(n_own_mt):
        slots = [dict(hist="carry", state=("statein",))] + [dict(hist="prev", state=("statein",)) for _ in range(3)]
        if m == n_own_mt - 1:
            slots[3]["out"] = 2
        layer_mt((m + 1) * T, slots, m < n_own_mt - 1, False)

    allres = [k for k in S.dcnt]
    final_waits = [(("d", k), v) for k, v in S.dcnt.items()]

    sems = {}
    for e in S.streams:
        sems[e] = es.enter_context(nc.semaphore(f"s_{e}"))
    for k in S.dcnt:
        sems[("d", k)] = es.enter_context(nc.semaphore(f"d_{k}"))
    block = es.enter_context(nc.Block())

    def replay(ename):
        def run(eng):
            for waits, fn, inc, dmak in S.streams[ename]:
                for k, v in waits[1:]:
                    eng.wait_ge(sems[k], v)
                ins = fn(eng)
                if waits:
                    ins._wait_ge(sems[waits[0][0]], waits[0][1])
                if dmak is not None:
                    ins.then_inc(sems[("d", dmak)], 16)
                elif inc:
                    ins.then_inc(sems[ename], 1)
            if ename == "sp":
                for k, v in final_waits:
                    eng.wait_ge(sems[k], v)
        return run

    block.tensor(replay("pe"))
    block.scalar(replay("act"))
    block.vector(replay("dve"))
    block.gpsimd(replay("pool"))
    block.sync(replay("sp"))
    es.close()
    return nc


def _fm(v, nch):
    return np.ascontiguousarray(np.asarray(v, np.float32).reshape(nch, 128).T)


def _fm_rows(a, nch):
    a = np.asarray(a, np.float32)
    R = a.shape[0]
    return np.ascontiguousarray(a.reshape(R, nch, 128).transpose(2, 1, 0).reshape(128, nch * R))


def _unfm_rows(b, nch, R):
    return np.ascontiguousarray(b.reshape(128, nch, R).transpose(2, 1, 0).reshape(R, nch * 128))


def _ws_tiles(w, starts, nkc):
    w = np.asarray(w, np.float32)
    out = np.empty((len(starts), 128, nkc * 128), np.float32)
    for i, s in enumerate(starts):
        out[i] = w[:nkc * 128, s:s + 128].reshape(nkc, 128, 128).transpose(1, 0, 2).reshape(128, nkc * 128)
    return out


def _tm_tiles(w, nkc, piece):
    w = np.asarray(w, np.float32)
    npiece = nkc // piece
    out = np.empty((4 * npiece, 128, piece * 512), np.float32)
    for n in range(4):
        for pc in range(npiece):
            blk = w[pc * piece * 128:(pc + 1) * piece * 128, n * 512:(n + 1) * 512]
            out[n * npiece + pc] = blk.reshape(piece, 128, 512).transpose(1, 0, 2).reshape(128, piece * 512)
    return out


def _tile_weights(p):
    starts = ([C_AVAL + 128 * i for i in range(16)] + [C_AGATE + 128 * i for i in range(16)]
              + [C_Z + 128 * i for i in range(32)] + [C_XBC + 128 * i for i in range(48)] + [C_DT]
              + [C_GA + 128 * i for i in range(16)] + [C_GB + 128 * i for i in range(16)])
    w_in = np.asarray(p["w_in"][0], np.float32)
    w_in_pad = np.concatenate([w_in, np.zeros((D, 128), np.float32)], 1)
    return {
        "w_in_t": _ws_tiles(w_in_pad, starts, KC),
        "w_a_t": _ws_tiles(p["w_a_out"][0], [128 * i for i in range(16)], KC),
        "w_b_t": _ws_tiles(p["w_b_out"][0], [128 * i for i in range(16)], 32),
        "w_o_t": _tm_tiles(p["w_o"][0], KC, 4),
        "w_up_t": _ws_tiles(p["w_up"][0], [128 * i for i in range(88)], KC),
        "w_dn_t": _tm_tiles(p["w_down"][0], NJ, 4),
    }


def _consts():
    cs = np.zeros((128, NCS), np.float32)
    cs[:, 0:128] = np.eye(128, dtype=np.float32)
    s = np.arange(64)
    cs[0:64, 128:192] = (s[:, None] <= s[None, :]).astype(np.float32)
    cs[0:64, 192:256] = np.where(s[None, :] >= s[:, None], 0.0, NEG).astype(np.float32)
    cs[:, 256:384] = 1.0
    return cs


def _pack_params(p, k, n_cores):
    pf = np.zeros((128, NPF), np.float32)

    def put(name, arr):
        o, w = PF[name]
        arr = np.asarray(arr, np.float32)
        pf[0:arr.shape[0], o:o + w] = arr.reshape(arr.shape[0], w)

    put("npre", _fm(p["norm_mix_pre"][0], 16))
    put("caw", _fm_rows(p["conv_a_w"][0], 16))
    put("cab", _fm(p["conv_a_b"][0], 16))
    put("lng", _fm(p["ln_a_g"][0], 16))
    put("lnb", _fm(p["ln_a_b"][0], 16))
    put("bg", _fm(p["b_gate"][0], 32))
    put("cbw", _fm_rows(p["conv_b_w"][0], 48))
    put("cbb", _fm(p["conv_b_b"][0], 48))
    put("sng", _fm(p["ssd_norm_g"][0], 32))
    put("nfpre", _fm(p["norm_ffn_pre"][0], 16))
    put("fcw", _fm_rows(p["ffn_conv_w"][0], 88))
    put("fcb", _fm(p["ffn_conv_b"][0], 88))
    dsk = np.asarray(p["d_skip"][0], np.float32)
    d2 = np.zeros((128, 32), np.float32)
    for c in range(32):
        d2[0:64, c] = dsk[2 * c]
        d2[64:128, c] = dsk[2 * c + 1]
    put("dsk", d2)
    put("dtb", np.asarray(p["dt_bias"][0], np.float32).reshape(64, 1))
    put("alog", np.asarray(p["a_log"][0], np.float32).reshape(64, 1))
    put("mk", np.full((128, 1), 0.0 if k == 0 else 1.0, np.float32))
    sel = np.zeros((8, 8), np.float32)
    selm = np.zeros((8,), np.float32)
    for j in range(8):
        if j < k:
            selm[j] = 1.0
            for m in range(8):
                if j < m < k:
                    sel[j, m] = 1.0
    put("sel", np.broadcast_to(sel.reshape(1, 64), (128, 64)))
    put("selm", np.broadcast_to(selm.reshape(1, 8), (128, 8)))
    return pf


def _core_inputs(p, k, n_own_mt, with_pass1, n_cores, own_tokens):
    xp = np.asarray(p["x_prompt"], np.float32)[0]
    xs = np.asarray(p["x_sample"], np.float32)
    s0 = k * own_tokens

    def prows(a, b):
        out = np.zeros((b - a, D), np.float32)
        lo, hi = max(a, 0), max(b, 0)
        if hi > lo:
            out[lo - a:hi - a] = xp[lo:hi]
        return out

    xin = np.concatenate([xs[2 * k], xs[2 * k + 1], prows(s0 - 128, s0), prows(s0, s0 + own_tokens)], 0)
    m = {"xin": np.ascontiguousarray(xin)}
    if with_pass1:
        n1 = n_own_mt + 1
        e0 = s0 + own_tokens
        parts = [prows(s0 - 128, s0 + 128)]
        if n1 > 2:
            parts.append(prows(s0 + 128, s0 + 128 + 256 * (n1 - 2)))
        parts.append(prows(e0 - 256, e0))
        m["x1in"] = np.ascontiguousarray(np.concatenate(parts, 0))
    m["pf"] = _pack_params(p, k, n_cores)
    m["uh"] = np.stack([_fm_rows(p["cache_conv_a"][0, 2 * k + i], 16) for i in range(2)])
    m["xh"] = np.stack([_fm_rows(p["cache_conv_b"][0, 2 * k + i], 48) for i in range(2)])
    m["fh"] = np.stack([_fm_rows(p["cache_ffn_conv"][0, 2 * k + i], 88) for i in range(2)])
    m["st"] = np.stack([np.ascontiguousarray(np.asarray(p["state_ssd"][0, 2 * k + i], np.float32).reshape(DI, 128).T)
                        for i in range(2)])
    return m


_NC_CACHE = {}
LAST_RES = None


def run_cores(p, n_cores, n_own_mt, with_pass1):
    own_tokens = n_own_mt * T
    key = (n_cores, n_own_mt, with_pass1)
    if key not in _NC_CACHE:
        _NC_CACHE[key] = build(n_own_mt, with_pass1, n_cores)
    nc = _NC_CACHE[key]
    shared = {
        "cs": _consts(),
        "gam1": np.ascontiguousarray(np.broadcast_to(np.asarray(p["norm_mix_post"][0], np.float32)[None, :], (128, D))),
        "gam2": np.ascontiguousarray(np.broadcast_to(np.asarray(p["norm_ffn_post"][0], np.float32)[None, :], (128, D))),
    }
    shared.update(_tile_weights(p))
    in_maps = []
    for k in range(n_cores):
        m = _core_inputs(p, k, n_own_mt, with_pass1, n_cores, own_tokens)
        m.update(shared)
        in_maps.append(m)
    if os.environ.get("KTRACE", "0") == "1":
        res = run_bass_kernel_spmd(nc, in_maps, core_ids=list(range(n_cores)), trace=True)
        print("KTRACE exec_time_ns", res.exec_time_ns)
        global LAST_RES
        LAST_RES = res
    else:
        res = run_bass_kernel_spmd(nc, in_maps, core_ids=list(range(n_cores)))
    return res.results


def assemble(results, n_cores, own_tokens):
    nseq = 2 * n_cores
    y_prompt = np.zeros((1, n_cores * own_tokens, D), np.float32)
    y_sample = np.zeros((nseq, 64, D), np.float32)
    ca_s = np.zeros((1, nseq, 30, D), np.float32)
    cb_s = np.zeros((1, nseq, 3, DXBC), np.float32)
    ss_s = np.zeros((1, nseq, NH, 64, 128), np.float32)
    cf_s = np.zeros((1, nseq, 2, 2 * DFF), np.float32)
    for k, r in enumerate(results):
        yo = np.asarray(r["yout"]).reshape(-1, D)
        r = dict(r)
        for nm, w in (("ouh", 480), ("oxh", 144), ("ofh", 176)):
            r[nm] = np.asarray(r[nm]).reshape(3, 128, w)
        y_sample[2 * k] = yo[0:64]
        y_sample[2 * k + 1] = yo[64:128]
        y_prompt[0, k * own_tokens:(k + 1) * own_tokens] = yo[256:256 + own_tokens]
        for i in range(2):
            ca_s[0, 2 * k + i] = _unfm_rows(r["ouh"][i], 16, 30)
            cb_s[0, 2 * k + i] = _unfm_rows(r["oxh"][i], 48, 3)
            cf_s[0, 2 * k + i] = _unfm_rows(r["ofh"][i], 88, 2)
            ss_s[0, 2 * k + i] = np.asarray(r["ost"]).reshape(3, 128, DI)[i].T.reshape(NH, 64, 128)
    r = dict(results[-1])
    for nm, w in (("ouh", 480), ("oxh", 144), ("ofh", 176)):
        r[nm] = np.asarray(r[nm]).reshape(3, 128, w)
    ca_p = _unfm_rows(r["ouh"][2], 16, 30)[None, None]
    cb_p = _unfm_rows(r["oxh"][2], 48, 3)[None, None]
    cf_p = _unfm_rows(r["ofh"][2], 88, 2)[None, None]
    ss_p = np.ascontiguousarray(np.asarray(r["ost"]).reshape(3, 128, DI)[2].T.reshape(NH, 64, 128))[None, None]
    return (y_prompt, y_sample, ca_p, cb_p, ss_p, cf_p, ca_s, cb_s, ss_s, cf_s)


def kernel(**inputs):
    n_cores = 8
    n_own_mt = 2048 // T
    results = run_cores(inputs, n_cores, n_own_mt, True)
    return assemble(results, n_cores, n_own_mt * T)
```

```python
import contextlib
import os
STOP = float(os.environ.get("KSTOP", "99"))
SKIPY = os.environ.get("KSKIPY", "0") == "1"
import numpy as np
import concourse.bass as bass
import concourse.mybir as mybir
from concourse.bass_utils import run_bass_kernel_spmd

F32 = mybir.dt.float32
BF16 = mybir.dt.bfloat16
AF = mybir.ActivationFunctionType
ALU = mybir.AluOpType

D = 2048
KC = 16
DI = 4096
DXBC = 6144
NH = 64
DFF = 5632
NJ = 44
EPS = 1e-6
C_AVAL, C_AGATE, C_Z, C_XBC, C_DT, C_GA, C_GB = 0, 2048, 4096, 8192, 14336, 14400, 16448
DINP = 18496
T = 256
NS = 4
NSUB = 2
L = 64
NEG = -30000.0

PF = {}
_o = 0
for _n, _w in [("npre", 16), ("caw", 16 * 31), ("cab", 16), ("lng", 16), ("lnb", 16), ("bg", 32),
               ("cbw", 48 * 4), ("cbb", 48), ("sng", 32), ("nfpre", 16), ("fcw", 88 * 3), ("fcb", 88),
               ("dsk", 32), ("dtb", 1), ("alog", 1), ("mk", 1), ("sel", 64), ("selm", 8)]:
    PF[_n] = (_o, _w)
    _o += _w
NPF = _o
CS = {"ident": (0, 128), "U": (128, 64), "negm": (192, 64), "ones": (256, 128)}
NCS = 384


class Sched:
    def __init__(self):
        self.streams = {e: [] for e in ("pe", "act", "dve", "pool", "sp")}
        self.cnt = {e: 0 for e in self.streams}
        self.seen = {e: {} for e in self.streams}
        self.res_w = {}
        self.res_r = {}
        self.dcnt = {}
        self.rotc = {}

    def rot(self, name, n):
        i = self.rotc.get(name, 0)
        self.rotc[name] = i + 1
        return i % n

    def op(self, eng, fn, reads=(), writes=(), inc=True, dma=None):
        reads = list(reads)
        writes = list(writes)
        ps_reads = [r for r in reads if isinstance(r, tuple) and len(r) == 2 and r[0] == "ps"]
        reads = [r for r in reads if r not in ps_reads]
        for r in ps_reads:
            if r not in writes:
                writes.append(r)
        waits = {}

        def need(t):
            if t is None:
                return
            k, v = t
            if k == "pe" and eng == "pe":
                return
            if self.seen[eng].get(k, 0) >= v:
                return
            waits[k] = max(waits.get(k, 0), v)

        for r in reads:
            need(self.res_w.get(r))
        for w in writes:
            need(self.res_w.get(w))
            for k, v in self.res_r.get(w, {}).items():
                need((k, v))
        for k, v in waits.items():
            self.seen[eng][k] = v
        if dma is not None:
            self.dcnt[dma] = self.dcnt.get(dma, 0) + 16
            tick = (("d", dma), self.dcnt[dma])
            inc = False
        elif inc:
            self.cnt[eng] += 1
            tick = (eng, self.cnt[eng])
        else:
            assert eng == "pe"
            tick = (eng, self.cnt[eng] + 1)
        for r in reads:
            d = self.res_r.setdefault(r, {})
            d[tick[0]] = max(d.get(tick[0], 0), tick[1])
        for w in writes:
            self.res_w[w] = tick
            self.res_r[w] = {}
        self.streams[eng].append((list(waits.items()), fn, inc, dma))


def build(n_own_mt, with_pass1, n_cores):
    nc = bass.Bass("TRN2", target_bir_lowering=False)
    NMT = 1 + n_own_mt
    NTOK = NMT * T
    NOWN = n_own_mt * T
    S = Sched()
    es = contextlib.ExitStack()

    def din(name, shape, dt=F32):
        return nc.dram_tensor(name, list(shape), dt, kind="ExternalInput").ap()

    def dout(name, shape, dt=F32):
        return nc.dram_tensor(name, list(shape), dt, kind="ExternalOutput").ap()

    xin = din("xin", [NTOK, D])
    x1in = din("x1in", [(n_own_mt + 1) * T, D]) if with_pass1 else None
    pf_d = din("pf", [128, NPF])
    cs_d = din("cs", [128, NCS])
    gam1_d = din("gam1", [128, D])
    gam2_d = din("gam2", [128, D])
    uh_d = din("uh", [2, 128, 16 * 30])
    xh_d = din("xh", [2, 128, 48 * 3])
    fh_d = din("fh", [2, 128, 88 * 2])
    st_d = din("st", [2, 128, DI])
    w_in_d = din("w_in_t", [145, 128, KC * 128])
    w_a_d = din("w_a_t", [16, 128, KC * 128])
    w_b_d = din("w_b_t", [16, 128, 32 * 128])
    w_o_d = din("w_o_t", [16, 128, 4 * 512])
    w_up_d = din("w_up_t", [88, 128, KC * 128])
    w_dn_d = din("w_dn_t", [44, 128, 4 * 512])
    yout = dout("yout", [NTOK, D])
    ouh = dout("ouh", [3, 128, 16 * 30])
    oxh = dout("oxh", [3, 128, 48 * 3])
    ofh = dout("ofh", [3, 128, 88 * 2])
    ost = dout("ost", [3, 128, DI])
    if with_pass1:
        cc_in = nc.dram_tensor("cc_in", [128, DI + NH], F32)
        cc_out = nc.dram_tensor("cc_out", [n_cores * 128, DI + NH], F32)

    def sb(name, f, dt=F32):
        return es.enter_context(nc.sbuf_tensor(name, [128, f], dt))

    hT = sb("hT", KC * T, BF16)
    RB = sb("RB", 48 * T, BF16)
    mrg = sb("mrg", KC * T, BF16)
    xmid = sb("xmid", NSUB * D)
    WS = [sb(f"ws{i}", 4096, BF16) for i in range(4)]
    ST = sb("ST", DI)
    STb = sb("STb", DI, BF16)
    xt = [sb(f"xt{i}", D) for i in range(2)]
    htok = sb("htok", D, BF16)
    ubuf = [sb(f"ubuf{i}", NS * 94) for i in range(2)]
    xbuf = [sb(f"xbuf{i}", NS * 67) for i in range(2)]
    upg = [sb(f"upg{i}", NS * 66) for i in range(2)]
    upv = [sb(f"upv{i}", NS * 66) for i in range(2)]
    NTMP = 5
    tmp = [sb(f"tmp{i}", 512) for i in range(NTMP)]
    pft = sb("pft", NPF)
    cst = sb("cst", NCS)
    identb = sb("identb", 128, BF16)
    gam = sb("gam", D)
    hu = sb("hu", 16 * 30)
    hx = sb("hx", 48 * 3)
    hf = sb("hf", 88 * 2)
    dtT = sb("dtT", 2 * T)
    rstdb = sb("rstdb", T)
    lnm = sb("lnm", 3 * T)
    small = sb("small", 64)
    xtok = sb("xtok", DI, BF16)
    btok = sb("btok", 1024, BF16)
    dtk = sb("dtk", 5 * 64)
    Rg = [sb(f"Rg{i}", 512) for i in range(2)]
    EB = [sb(f"EB{i}", 512) for i in range(2)]
    Lm = [sb(f"Lm{i}", 512) for i in range(2)]
    MT_ = [sb(f"MTt{i}", 512, BF16) for i in range(2)]
    CE = [sb(f"CE{i}", 512, BF16) for i in range(2)]
    XW = [sb(f"XW{i}", 512, BF16) for i in range(2)]
    Ssb = [sb(f"Ssb{i}", 64) for i in range(2)]
    LAM = sb("LAM", 64)
    if with_pass1:
        lamall = tmp[0]
        wj = tmp[1]
    PS = [es.enter_context(nc.psum_tensor(f"ps{i}", [128, 512], F32)) for i in range(8)]

    def pfc(name, c=0, n=1, rows=128):
        o, _ = PF[name]
        return pft[0:rows, o + c:o + c + n]

    def csv(name, rows=128, n=None):
        o, w = CS[name]
        return cst[0:rows, o:o + (n or w)]

    def V(t, f, p0, npart, off, dims):
        return bass.AP(t, p0 * f + off, [[f, npart]] + [list(d) for d in dims])

    def dma(q, out, in_, reads, writes, sem):
        S.op(q, lambda e, out=out, in_=in_: e.dma_start(out=out, in_=in_), reads=reads, writes=writes, dma=sem)

    def act(out, in_, func, reads, writes, bias=None, scale=None, accum=None):
        kw = {}
        if bias is not None:
            kw["bias"] = bias
        if scale is not None:
            kw["scale"] = scale
        if accum is not None:
            kw["accum_out"] = accum
        S.op("act", lambda e: e.activation(out=out, in_=in_, func=func, **kw), reads=reads, writes=writes)

    def tt(out, in0, in1, op, reads, writes, eng="dve"):
        S.op(eng, lambda e: e.tensor_tensor(out=out, in0=in0, in1=in1, op=op), reads=reads, writes=writes)

    def rsqrt(out, in_, scale, reads, writes):
        act(out, in_, AF.Sqrt, list(reads) + ["epsc"], writes, bias=small[0:out.shape[0], 40:41], scale=scale)
        S.op("dve", lambda e: e.reciprocal(out=out, in_=out), reads=writes, writes=writes)

    def ts(out, in0, s1, s2, op0, op1, reads, writes, eng="dve"):
        if s2 is None:
            S.op(eng, lambda e: e.tensor_scalar(out=out, in0=in0, scalar1=s1, scalar2=None, op0=op0),
                 reads=reads, writes=writes)
        else:
            S.op(eng, lambda e: e.tensor_scalar(out=out, in0=in0, scalar1=s1, scalar2=s2, op0=op0, op1=op1),
                 reads=reads, writes=writes)

    def stt(out, in0, sc, in1, op0, op1, reads, writes, eng="dve"):
        S.op(eng, lambda e: e.scalar_tensor_tensor(out=out, in0=in0, scalar=sc, in1=in1, op0=op0, op1=op1),
             reads=reads, writes=writes)

    def mm(out, lhsT, rhs, start, stop, reads, writes, last):
        S.op("pe", lambda e: e.matmul(out, lhsT, rhs, start=start, stop=stop), reads=reads, writes=writes, inc=last)

    def tr(out, in_, ident, reads, writes, last):
        S.op("pe", lambda e: e.transpose(out, in_, ident), reads=reads, writes=writes, inc=last)

    def psum():
        i = S.rot("ps", 7)
        return PS[i], ("ps", i)

    def tmpb():
        i = S.rot("tmp", NTMP)
        return tmp[i], ("tmp", i)

    def wload(wd, tile, F, ncols=128):
        i = S.rot("ws", 4)
        slot = WS[i]
        dma("pool", slot[:, 0:F], wd[tile], [], [("ws", i)], f"w{i}")
        return slot, ("ws", i), ncols

    def wl(slotinfo, kc, c0, n):
        slot, _, ncols = slotinfo
        return slot[:, kc * ncols + c0: kc * ncols + c0 + n]

    RBk = lambda page: ("RB", page)
    xsT = lambda c: RB[:, c * T:(c + 1) * T]
    BCT = lambda c: RB[:, (32 + c) * T:(33 + c) * T]
    ucv = lambda c: RB[:, 2 * c * T:(2 * c + 2) * T].bitcast(F32)
    uav = lambda c: RB[:, (32 + c) * T:(33 + c) * T]
    gTv = lambda j: RB[:, j * T:(j + 1) * T]
    hTv = lambda kc: hT[:, kc * T:(kc + 1) * T]
    mrgv = lambda c: mrg[:, c * T:(c + 1) * T]

    dma("sp", pft[:, :], pf_d, [], ["pft"], "c0")
    dma("sp", cst[:, :], cs_d, [], ["cst"], "c1")
    act(identb[:, :], csv("ident"), AF.Copy, ["cst"], ["identb"])
    act(small[0:64, 0:1], pfc("alog", rows=64), AF.Exp, ["pft"], ["small"])
    ts(small[0:64, 0:1], small[0:64, 0:1], -1.0, None, ALU.mult, None, ["small"], ["small"])
    aneg = small[0:64, 0:1]
    S.op("dve", lambda e: e.memset(small[:, 40:41], EPS), reads=[], writes=["epsc"])
    S.op("dve", lambda e: e.memset(small[:, 41:42], 1.0), reads=["epsc"], writes=["epsc"])
    identf = csv("ident")
    Uf = csv("U", rows=64)
    negm = csv("negm", rows=64)
    onesf = csv("ones", rows=64)
    ones128 = csv("ones")

    def proj(wd, tile0, ntiles, rhs_fn, nkc, evac, rhs_reads, M=128):
        for b in range(ntiles):
            wsl = wload(wd, tile0 + b, nkc * 128)
            ps, psk = psum()
            for kc in range(nkc):
                mm(ps[0:M, 0:T], wl(wsl, kc, 0, M), rhs_fn(kc), kc == 0, kc == nkc - 1,
                   [wsl[1]] + rhs_reads(kc), [psk], kc == nkc - 1)
            evac(b, ps, psk)

    def hist_ops(buf, bk, W1, width, c, nch, slots, hcarry, hck, hin_d, osb, osk, odr, save_carry, mask_carry):
        SW = W1 + 64
        merged_prev = all(s_["hist"] == "prev" for s_ in slots[1:])
        if merged_prev:
            act(V(buf, NS * SW, 0, 128, SW, [[SW, NS - 1], [1, W1]]), V(buf, NS * SW, 0, 128, 64, [[SW, NS - 1], [1, W1]]),
                AF.Copy, [bk], [bk])
        for j, sl in enumerate(slots):
            dst = buf[:, j * SW: j * SW + W1]
            src = sl["hist"]
            if src == "prev" and merged_prev:
                continue
            if src == "prev":
                act(dst, buf[:, (j - 1) * SW + 64:(j - 1) * SW + 64 + W1], AF.Copy, [bk], [bk])
            elif src == "carry":
                act(dst, hcarry[:, c * W1:(c + 1) * W1], AF.Copy, [hck], [bk])
            elif src[0] == "cache":
                dma("sp", dst, hin_d[src[1], :, c * W1:(c + 1) * W1], [], [bk], f"hin_{bk[0]}{bk[1]}_{j}")
            else:
                S.op("dve", lambda e, dst=dst: e.memset(dst, 0.0), reads=[], writes=[bk])
        for j, sl in enumerate(slots):
            if sl.get("out") is not None:
                tail = buf[:, j * SW + 64: j * SW + 64 + W1]
                dma("sp", odr[sl["out"], :, c * W1:(c + 1) * W1], tail, [bk], [], f"hout_{bk[0]}{bk[1]}")
        if save_carry:
            tail = buf[:, (NS - 1) * SW + 64:(NS - 1) * SW + 64 + W1]
            if mask_carry:
                ts(hcarry[:, c * W1:(c + 1) * W1], tail, pfc("mk"), None, ALU.mult, None, [bk, "pft"], [hck])
            else:
                act(hcarry[:, c * W1:(c + 1) * W1], tail, AF.Copy, [bk], [hck])

    def conv(buf, bk, W1, wname, bname, c, out_ap, outk, nacc):
        SW = W1 + 64
        ntap = W1 + 1
        wo = PF[wname][0]
        o3 = lambda ap: ap.rearrange("p (s t) -> p s t", s=NS)
        accs = []
        per = (ntap + nacc - 1) // nacc
        for a in range(nacc):
            if a == nacc - 1:
                acc, acck = out_ap, outk
            else:
                tb, tk = tmpb()
                acc, acck = tb[:, 0:T], [tk]
            taps = list(range(a * per, min(ntap, (a + 1) * per)))
            for n, w in enumerate(taps):
                src = V(buf, NS * SW, 0, 128, w, [[SW, NS], [1, 64]])
                wcol = pft[:, wo + c * ntap + w: wo + c * ntap + w + 1]
                if n == 0:
                    if a == 0:
                        ts(o3(acc), src, wcol, pfc(bname, c), ALU.mult, ALU.add, [bk, "pft"], acck)
                    else:
                        ts(o3(acc), src, wcol, None, ALU.mult, None, [bk, "pft"], acck)
                else:
                    stt(o3(acc), src, wcol, o3(acc), ALU.mult, ALU.add, [bk, "pft"] + acck, acck)
            accs.append((acc, acck))
        for a in range(nacc - 1):
            tt(out_ap, out_ap, accs[a][0], ALU.add, outk + accs[a][1], outk)

    def norm_to_T(src_rows_fn, dstT, dstk, gname, src_is_dram):
        for s in range(NSUB):
            if src_is_dram:
                xi = S.rot("xt", 2)
                xb, xk = xt[xi], ("xt", xi)
                dma("sp", xb[:, :], src_rows_fn(s), [], [xk], f"xin{xi}")
                src, srck = xb[:, :], [xk]
            else:
                src, srck = src_rows_fn(s), ["xmid"]
            S.op("dve", lambda e: e.memset(small[:, 8:12], 0.0), reads=[], writes=["ss"])
            for n in range(4):
                act(sb_junk[:, 0:512], src[:, n * 512:(n + 1) * 512], AF.Square, srck, ["junk", "ss"],
                    accum=small[:, 8 + n:9 + n])
            ts(small[:, 12:13], small[:, 8:9], small[:, 9:10], None, ALU.add, None, ["ss"], ["ss2"])
            ts(small[:, 12:13], small[:, 12:13], small[:, 10:11], None, ALU.add, None, ["ss", "ss2"], ["ss2"])
            ts(small[:, 12:13], small[:, 12:13], small[:, 11:12], None, ALU.add, None, ["ss", "ss2"], ["ss2"])
            rsqrt(small[:, 12:13], small[:, 12:13], 1.0 / D, ["ss2"], ["ss2"])
            ts(htok[:, :], src, small[:, 12:13], None, ALU.mult, None, srck + ["ss2"], ["htok"])
            for c4 in range(4):
                ps, psk = psum()
                pb = ps[:].bitcast(BF16)
                for q in range(4):
                    c = c4 * 4 + q
                    tr(pb[:, q * 128:(q + 1) * 128], htok[:, c * 128:(c + 1) * 128], identb[:, :],
                       ["htok", "identb"], [psk], q == 3)
                for q in range(4):
                    c = c4 * 4 + q
                    act(dstT[:, c * T + s * 128: c * T + (s + 1) * 128], pb[:, q * 128:(q + 1) * 128], AF.Copy,
                        [psk, "pft"], [(dstk, c)], scale=pfc(gname, c))

    sb_junk = sb("junk", 512)

    def ssd_prep_T(j, masked):
        t0 = j * L
        for b in range(4):
            ps, psk = psum()
            pb = ps[:].bitcast(BF16)
            for q in range(8):
                c = b * 8 + q
                tr(pb[0:64, q * 128:(q + 1) * 128], xsT(c)[:, t0:t0 + L], identb[:, :], [RBk(c), "identb"], [psk], q == 7)
            act(xtok[0:64, b * 1024:(b + 1) * 1024], pb[0:64, 0:1024], AF.Copy, [psk], [("xtok", b)])
        ps, psk = psum()
        pb = ps[:].bitcast(BF16)
        for g in range(8):
            tr(pb[0:64, g * 128:(g + 1) * 128], BCT(g)[:, t0:t0 + L], identb[:, :], [RBk(32 + g), "identb"], [psk], g == 7)
        act(btok[0:64, :], pb[0:64, 0:1024], AF.Copy, [psk], ["btok"])
        ps, psk = psum()
        tr(ps[0:64, 0:64], dtT[0:64, t0:t0 + L], identf[0:64, 0:64], ["dtT", "cst"], [psk], False)
        tr(ps[0:64, 64:128], dtT[0:64, T + t0:T + t0 + L], identf[0:64, 0:64], ["dtT", "cst"], [psk], True)
        if masked:
            ts(dtk[0:64, 0:128], ps[0:64, 0:128], pfc("mk", rows=64), None, ALU.mult, None, [psk, "pft"], ["dtk"])
        else:
            act(dtk[0:64, 0:128], ps[0:64, 0:128], AF.Copy, [psk], ["dtk"])
        ps, psk = psum()
        mm(ps[0:64, 0:64], Uf[:, 0:64], dtk[0:64, 64:128], True, True, ["cst", "dtk"], [psk], False)
        mm(ps[0:64, 64:128], onesf[:, 0:64], dtk[0:64, 64:128], True, True, ["cst", "dtk"], [psk], True)
        act(dtk[0:64, 128:192], ps[0:64, 0:64], AF.Copy, [psk], ["dtk2"])
        tt(dtk[0:64, 192:256], ps[0:64, 64:128], dtk[0:64, 128:192], ALU.subtract, [psk, "dtk2"], ["dtk3"])
        act(dtk[0:64, 192:256], dtk[0:64, 192:256], AF.Exp, ["dtk3"], ["dtk3"])
        tt(dtk[0:64, 256:320], dtk[0:64, 192:256], dtk[0:64, 0:64], ALU.mult, ["dtk3", "dtk"], ["dtk4"])

    def ssd_slot(j, masked, want_y, acc_lam):
        t0 = j * L
        ssd_prep_T(j, masked)
        if STOP <= 4.2:
            return
        for g in range(8):
            i2 = S.rot("grp", 2)
            R, EBg, Lmg, MTg, CEg, XWg, Sg = Rg[i2], EB[i2], Lm[i2], MT_[i2], CE[i2], XW[i2], Ssb[i2]
            k = lambda n: (n, i2)
            dta_b = V(dtk, 320, 0, 64, 64 + 8 * g, [[1, 8], [0, 64]])
            U_b = V(cst, NCS, 0, 64, CS["U"][0], [[0, 8], [1, 64]])
            tt(V(R, 512, 0, 64, 0, [[64, 8], [1, 64]]), dta_b, U_b, ALU.mult, ["dtk", "cst"], [k("R")], eng="pool")
            psc, psck = psum()
            mm(psc[:, 0:512], onesf[:, 0:128], R[0:64, 0:512], True, True, ["cst", k("R")], [psck], True)
            if acc_lam:
                lastv = V(psc, 512, 0, 128, 63, [[64, 8]])
                tt(LAM[:, 8 * g:8 * g + 8], LAM[:, 8 * g:8 * g + 8], lastv, ALU.add, [psck, "LAM"], ["LAM"])
            act(EBg[:, 0:512], psc[:, 0:512], AF.Exp, [psck], [k("EB")])
            if STOP <= 4.4:
                continue
            if want_y and not SKIPY:
                cum_b = V(dtk, 320, 0, 64, 128 + 8 * g, [[1, 8], [0, 64]])
                tb, tk = tmpb()
                seg = V(tb, 512, 0, 64, 0, [[64, 8], [1, 64]])
                tt(seg, V(psc, 512, 0, 64, 0, [[64, 8], [1, 64]]), cum_b, ALU.subtract, [psck, "dtk2"], [tk])
                negm_b = V(cst, NCS, 0, 64, CS["negm"][0], [[0, 8], [1, 64]])
                tt(seg, seg, negm_b, ALU.add, [tk, "cst"], [tk])
                act(Lmg[0:64, 0:512], tb[0:64, 0:512], AF.Exp, [tk], [k("Lm")])
                if STOP <= 4.5:
                    continue
                pss, pssk = psum()
                mm(pss[0:64, 0:64], BCT(g)[:, t0:t0 + L], BCT(8 + g)[:, t0:t0 + L], True, True,
                   [RBk(32 + g), RBk(40 + g)], [pssk], True)
                act(Sg[0:64, 0:64], pss[0:64, 0:64], AF.Copy, [pssk], [k("S")])
                dt_b = V(dtk, 320, 0, 64, 8 * g, [[1, 8], [0, 64]])
                Lm3 = V(Lmg, 512, 0, 64, 0, [[64, 8], [1, 64]])
                tt(Lm3, Lm3, dt_b, ALU.mult, [k("Lm"), "dtk"], [k("Lm")], eng="pool")
                S_b = V(Sg, 64, 0, 64, 0, [[0, 8], [1, 64]])
                tt(V(MTg, 512, 0, 64, 0, [[64, 8], [1, 64]]), Lm3, S_b, ALU.mult, [k("Lm"), k("S")], [k("MT")])
                C_b = V(RB, 48 * T, 0, 128, (40 + g) * T + t0, [[0, 8], [1, 64]])
                tt(V(CEg, 512, 0, 128, 0, [[64, 8], [1, 64]]), V(EBg, 512, 0, 128, 0, [[64, 8], [1, 64]]), C_b,
                   ALU.mult, [k("EB"), RBk(40 + g)], [k("CE")], eng="pool")
                if STOP <= 4.6:
                    continue
                psy, psyk = psum()
                for q in range(4):
                    cidx = 4 * g + q
                    for hh in range(2):
                        o = psy[:, (2 * q + hh) * 64:(2 * q + hh + 1) * 64]
                        mm(o, xtok[0:64, cidx * 128:(cidx + 1) * 128], MTg[0:64, (2 * q + hh) * 64:(2 * q + hh + 1) * 64],
                           True, False, [("xtok", cidx // 8), k("MT")], [psyk], False)
                        mm(o, STb[:, cidx * 128:(cidx + 1) * 128], CEg[:, (2 * q + hh) * 64:(2 * q + hh + 1) * 64],
                           False, True, [("STb", g), k("CE")], [psyk], q == 3 and hh == 1)
                if STOP <= 4.7:
                    continue
                for hh in range(2):
                    p0 = 64 * hh
                    xs3 = V(RB, 48 * T, p0, 64, 4 * g * T + t0, [[T, 4], [1, 64]])
                    d_b = V(pft, NPF, p0, 64, PF["dsk"][0] + 4 * g, [[1, 4], [0, 64]])
                    y_ps = V(psy, 512, p0, 64, hh * 64, [[128, 4], [1, 64]])
                    rk = [RBk(4 * g + q) for q in range(4)]
                    tt(xs3, xs3, d_b, ALU.mult, rk + ["pft"], rk)
                    tt(xs3, xs3, y_ps, ALU.add, rk + [psyk], rk)
                if STOP <= 4.8:
                    continue
            w_b = V(dtk, 320, 0, 64, 256 + 8 * g, [[1, 8], [0, 64]])
            tt(V(XWg, 512, 0, 64, 0, [[64, 8], [1, 64]]), V(xtok, DI, 0, 64, 512 * g, [[64, 8], [1, 64]]), w_b,
               ALU.mult, [("xtok", g // 2), "dtk4"], [k("XW")], eng="pool")
            pst, pstk = psum()
            mm(pst[:, 0:512], btok[0:64, g * 128:(g + 1) * 128], XWg[0:64, 0:512], True, True, ["btok", k("XW")], [pstk], True)
            st3 = V(ST, DI, 0, 128, 512 * g, [[64, 8], [1, 64]])
            eb_last = V(EBg, 512, 0, 128, 63, [[64, 8], [0, 64]])
            tt(st3, st3, eb_last, ALU.mult, [("ST", g), k("EB")], [("ST", g)])
            tt(ST[:, 512 * g:512 * g + 512], ST[:, 512 * g:512 * g + 512], pst[:, 0:512], ALU.add, [("ST", g), pstk], [("ST", g)])
            if want_y:
                act(STb[:, 512 * g:512 * g + 512], ST[:, 512 * g:512 * g + 512], AF.Copy, [("ST", g)], [("STb", g)])

    def load_state(src_ap, reads=()):
        for g in range(8):
            dma("sp", ST[:, 512 * g:512 * g + 512], src_ap[:, 512 * g:512 * g + 512], list(reads), [("ST", g)], f"stin{g}")
            act(STb[:, 512 * g:512 * g + 512], ST[:, 512 * g:512 * g + 512], AF.Copy, [("ST", g)], [("STb", g)])

    def zero_state():
        for g in range(8):
            S.op("dve", lambda e, g=g: e.memset(ST[:, 512 * g:512 * g + 512], 0.0), reads=[], writes=[("ST", g)])
            act(STb[:, 512 * g:512 * g + 512], ST[:, 512 * g:512 * g + 512], AF.Copy, [("ST", g)], [("STb", g)])

    def store_state(dst_ap):
        for g in range(8):
            dma("sp", dst_ap[:, 512 * g:512 * g + 512], ST[:, 512 * g:512 * g + 512], [("ST", g)], [], f"stout{g}")

    def branchB_front(slots, save_carry, use_hist_outs):
        def evac_x(c, ps, psk):
            bi = S.rot("xbuf", 2)
            buf, bk = xbuf[bi], ("xbuf", bi)
            act(V(buf, NS * 67, 0, 128, 3, [[67, NS], [1, 64]]), ps[:, 0:T].rearrange("p (s t) -> p s t", s=NS),
                AF.Copy, [psk], [bk])
            sl2 = slots if use_hist_outs else [dict(s_, out=None) for s_ in slots]
            hist_ops(buf, bk, 3, 4, c, 48, sl2, hx, "hx", xh_d, None, None, oxh, save_carry, False)
            tb, tk = tmpb()
            conv(buf, bk, 3, "cbw", "cbb", c, tb[:, 0:T], [tk], 1)
            dst = xsT(c) if c < 32 else BCT(c - 32)
            act(dst, tb[:, 0:T], AF.Silu, [tk], [RBk(c)])
        proj(w_in_d, 64, 48, hTv, KC, evac_x, lambda kc: [("hT", kc)])

        def evac_dt(c, ps, psk):
            act(dtT[0:64, 0:T], ps[0:64, 0:T], AF.Exp, [psk, "pft"], ["dtT"], bias=pfc("dtb", rows=64))
            act(dtT[0:64, 0:T], dtT[0:64, 0:T], AF.Ln, ["dtT", "epsc"], ["dtT"], bias=small[0:64, 41:42])
            ts(dtT[0:64, T:2 * T], dtT[0:64, 0:T], aneg, None, ALU.mult, None, ["dtT", "small"], ["dtT"])
        proj(w_in_d, 112, 1, hTv, KC, evac_dt, lambda kc: [("hT", kc)], M=64)

    def layer_mt(row0, slots, save_carry, mask_up_carry):
        xrows = lambda s: xin[row0 + s * 128: row0 + (s + 1) * 128, :]
        norm_to_T(xrows, hT, "hT", "npre", True)
        if STOP <= 1:
            return
        def evac_a_pair():
            pass
        for c in range(16):
            wa = wload(w_in_d, c, KC * 128)
            wg = wload(w_in_d, 16 + c, KC * 128)
            psa, psak = psum()
            psg, psgk = psum()
            for kc in range(KC):
                mm(psa[:, 0:T], wl(wa, kc, 0, 128), hTv(kc), kc == 0, kc == KC - 1, [wa[1], ("hT", kc)], [psak], kc == KC - 1)
            for kc in range(KC):
                mm(psg[:, 0:T], wl(wg, kc, 0, 128), hTv(kc), kc == 0, kc == KC - 1, [wg[1], ("hT", kc)], [psgk], kc == KC - 1)
            tb, tk = tmpb()
            act(tb[:, 0:T], psg[:, 0:T], AF.Sigmoid, [psgk], [tk])
            bi = S.rot("ubuf", 2)
            buf, bk = ubuf[bi], ("ubuf", bi)
            tt(V(buf, NS * 94, 0, 128, 30, [[94, NS], [1, 64]]), psa[:, 0:T].rearrange("p (s t) -> p s t", s=NS),
               tb[:, 0:T].rearrange("p (s t) -> p s t", s=NS), ALU.mult, [psak, tk], [bk])
            hist_ops(buf, bk, 30, 31, c, 16, slots, hu, "hu", uh_d, None, None, ouh, save_carry, False)
            conv(buf, bk, 30, "caw", "cab", c, ucv(c), [RBk(2 * c), RBk(2 * c + 1)], 2)
        if STOP <= 2:
            return
        psm, psmk = psum()
        pss, pssk = psum()
        for c in range(16):
            mm(psm[:, 0:T], ones128, ucv(c), c == 0, c == 15, ["cst", RBk(2 * c), RBk(2 * c + 1)], [psmk], c == 15)
        for c in range(16):
            tb, tk = tmpb()
            act(tb[:, 0:T], ucv(c), AF.Square, [RBk(2 * c), RBk(2 * c + 1)], [tk])
            mm(pss[:, 0:T], ones128, tb[:, 0:T], c == 0, c == 15, ["cst", tk], [pssk], True)
        if STOP <= 2.3:
            return
        mean, rstd, nmr = lnm[:, 0:T], lnm[:, T:2 * T], lnm[:, 2 * T:3 * T]
        act(mean, psm[:, 0:T], AF.Copy, [psmk], ["lnm0"], scale=1.0 / D)
        tt(rstd, mean, mean, ALU.mult, ["lnm0"], ["lnm1"])
        stt(rstd, pss[:, 0:T], 1.0 / D, rstd, ALU.mult, ALU.subtract, [pssk, "lnm1"], ["lnm1"])
        rsqrt(rstd, rstd, 1.0, ["lnm1"], ["lnm1"])
        stt(nmr, mean, -1.0, rstd, ALU.mult, ALU.mult, ["lnm0", "lnm1"], ["lnm2"])
        for c in range(16):
            tb, tk = tmpb()
            tt(tb[:, 0:T], ucv(c), rstd, ALU.mult, [RBk(2 * c), RBk(2 * c + 1), "lnm1"], [tk])
            tt(tb[:, 0:T], tb[:, 0:T], nmr, ALU.add, [tk, "lnm2"], [tk])
            act(uav(c), tb[:, 0:T], AF.Silu, [tk, "pft"], [RBk(32 + c)], bias=pfc("lnb", c), scale=pfc("lng", c))
        if STOP <= 2.6:
            return
        for c in range(16):
            wy = wload(w_a_d, c, KC * 128)
            wg = wload(w_in_d, 113 + c, KC * 128)
            psy, psyk = psum()
            psg, psgk = psum()
            for kc in range(KC):
                mm(psy[:, 0:T], wl(wy, kc, 0, 128), uav(kc), kc == 0, kc == KC - 1, [wy[1], RBk(32 + kc)], [psyk], kc == KC - 1)
            for kc in range(KC):
                mm(psg[:, 0:T], wl(wg, kc, 0, 128), hTv(kc), kc == 0, kc == KC - 1, [wg[1], ("hT", kc)], [psgk], kc == KC - 1)
            tb, tk = tmpb()
            act(tb[:, 0:T], psg[:, 0:T], AF.Sigmoid, [psgk, "pft"], [tk], bias=pfc("bg", c))
            tt(mrgv(c), psy[:, 0:T], tb[:, 0:T], ALU.mult, [psyk, tk], [("mrg", c)])
        if STOP <= 3:
            return
        branchB_front(slots, save_carry, True)
        if STOP <= 4:
            return
        for j, sl in enumerate(slots):
            st = sl.get("state")
            if st is None:
                continue
            if st[0] == "load":
                load_state(st_d[st[1]])
            elif st[0] == "zero":
                zero_state()
            elif st[0] == "halo":
                halo_state()
            ssd_slot(j, sl.get("masked", False), True, False)
            if sl.get("out") is not None:
                store_state(ost[sl["out"]])
        if STOP <= 5:
            return
        psq, psqk = PS[7], ("ps", 7)
        for c0 in range(0, 32):
            wz = wload(w_in_d, 32 + c0, KC * 128)
            for bi in range(1):
                c = c0 + bi
                ps, psk = psum()
                for kc in range(KC):
                    mm(ps[:, 0:T], wl(wz, kc, bi * 128, 128), hTv(kc), kc == 0, kc == KC - 1, [wz[1], ("hT", kc)], [psk], kc == KC - 1)
                tb, tk = tmpb()
                act(tb[:, 0:T], ps[:, 0:T], AF.Silu, [psk], [tk])
                tt(tb[:, 0:T], tb[:, 0:T], xsT(c), ALU.mult, [tk, RBk(c)], [tk])
                tb2, tk2 = tmpb()
                act(tb2[:, 0:T], tb[:, 0:T], AF.Square, [tk], [tk2])
                mm(psq[:, 0:T], ones128, tb2[:, 0:T], c == 0, c == 31, ["cst", tk2], [psqk], True)
                ts(xsT(c), tb[:, 0:T], pfc("sng", c), None, ALU.mult, None, [tk, "pft"], [RBk(c)])
        rsqrt(rstdb[:, 0:T], psq[:, 0:T], 1.0 / DI, [psqk], ["rstdb"])
        for c in range(16):
            wy = wload(w_b_d, c, 32 * 128)
            wg = wload(w_in_d, 129 + c, KC * 128)
            psy, psyk = psum()
            psg, psgk = psum()
            for kc in range(32):
                mm(psy[:, 0:T], wl(wy, kc, 0, 128), xsT(kc), kc == 0, kc == 31, [wy[1], RBk(kc)], [psyk], kc == 31)
            for kc in range(KC):
                mm(psg[:, 0:T], wl(wg, kc, 0, 128), hTv(kc), kc == 0, kc == KC - 1, [wg[1], ("hT", kc)], [psgk], kc == KC - 1)
            tb, tk = tmpb()
            act(tb[:, 0:T], psg[:, 0:T], AF.Sigmoid, [psgk, "pft"], [tk], bias=pfc("bg", 16 + c))
            tb2, tk2 = tmpb()
            tt(tb2[:, 0:T], psy[:, 0:T], rstdb[:, 0:T], ALU.mult, [psyk, "rstdb"], [tk2])
            tt(tb2[:, 0:T], tb2[:, 0:T], tb[:, 0:T], ALU.mult, [tk2, tk], [tk2])
            tt(mrgv(c), mrgv(c), tb2[:, 0:T], ALU.add, [("mrg", c), tk2], [("mrg", c)])
        if STOP <= 6:
            return
        dma("sp", gam[:, :], gam1_d, [], ["gam"], "gam")
        tokmajor_out(w_o_d, KC, 4, lambda kc, s: mrg[:, kc * T + s * 128: kc * T + (s + 1) * 128],
                     lambda kc: [("mrg", kc)], lambda s: xmid[:, s * D:(s + 1) * D], ["xmid"], 10)
        for s in range(NSUB):
            xi = S.rot("xt", 2)
            xb, xk = xt[xi], ("xt", xi)
            dma("sp", xb[:, :], xrows(s), [], [xk], f"xin{xi}")
            finish_tok(xmid[:, s * D:(s + 1) * D], ["xmid"], 10 + 4 * s, xb[:, :], [xk])
        if STOP <= 7:
            return
        norm_to_T(lambda s: xmid[:, s * D:(s + 1) * D], hT, "hT", "nfpre", False)
        for j in range(NJ):
            wg = wload(w_up_d, j, KC * 128)
            wv = wload(w_up_d, NJ + j, KC * 128)
            psg, psgk = psum()
            psv, psvk = psum()
            for kc in range(KC):
                mm(psg[:, 0:T], wl(wg, kc, 0, 128), hTv(kc), kc == 0, kc == KC - 1, [wg[1], ("hT", kc)], [psgk], kc == KC - 1)
            for kc in range(KC):
                mm(psv[:, 0:T], wl(wv, kc, 0, 128), hTv(kc), kc == 0, kc == KC - 1, [wv[1], ("hT", kc)], [psvk], kc == KC - 1)
            res = []
            for (ps, psk, bufs, nm, cc) in ((psg, psgk, upg, "upg", j), (psv, psvk, upv, "upv", NJ + j)):
                bi = S.rot(nm, 2)
                buf, bk = bufs[bi], (nm, bi)
                act(V(buf, NS * 66, 0, 128, 2, [[66, NS], [1, 64]]), ps[:, 0:T].rearrange("p (s t) -> p s t", s=NS),
                    AF.Copy, [psk], [bk])
                hist_ops(buf, bk, 2, 3, cc, 88, slots, hf, "hf", fh_d, None, None, ofh, save_carry, mask_up_carry)
                tb, tk = tmpb()
                conv(buf, bk, 2, "fcw", "fcb", cc, tb[:, 0:T], [tk], 1)
                res.append((tb, tk))
            (tg, tgk), (tv, tvk) = res
            act(tg[:, 0:T], tg[:, 0:T], AF.Gelu_apprx_tanh, [tgk], [tgk])
            tt(gTv(j), tg[:, 0:T], tv[:, 0:T], ALU.mult, [tgk, tvk], [RBk(j)])
        if STOP <= 8:
            return
        dma("sp", gam[:, :], gam2_d, [], ["gam"], "gam")
        ybufs = []
        for s in range(NSUB):
            xi = S.rot("xt", 2)
            ybufs.append((xt[xi], ("xt", xi)))
        tokmajor_out(w_dn_d, NJ, 4, lambda kc, s: RB[:, kc * T + s * 128: kc * T + (s + 1) * 128],
                     lambda kc: [RBk(kc)], lambda s: ybufs[s][0][:, :], None, 20, outks=[[b[1]] for b in ybufs])
        for s in range(NSUB):
            yb, yk = ybufs[s]
            finish_tok(yb[:, :], [yk], 20 + 4 * s, xmid[:, s * D:(s + 1) * D], ["xmid"])
            dma("sp", yout[row0 + s * 128: row0 + (s + 1) * 128, :], yb[:, :], [yk], [], f"yout{yk[1]}")

    def tokmajor_out(wd, nkc, piece, lhs_fn, lhs_reads, dst_fn, dstk, sscol, outks=None):
        npiece = nkc // piece
        for s in range(NSUB):
            S.op("dve", lambda e, s=s: e.memset(small[:, sscol + 4 * s: sscol + 4 * s + 4], 0.0), reads=[],
                 writes=[("ssq", sscol, s)])
        for n in range(4):
            banks = [psum() for _ in range(NSUB)]
            for pc in range(npiece):
                wsl = wload(wd, n * npiece + pc, piece * 512, 512)
                for s in range(NSUB):
                    ps, psk = banks[s]
                    for kk in range(piece):
                        kc = pc * piece + kk
                        mm(ps[:, 0:512], lhs_fn(kc, s), wl(wsl, kk, 0, 512), kc == 0, kc == nkc - 1,
                           [wsl[1]] + lhs_reads(kc), [psk], kk == piece - 1)
            for s in range(NSUB):
                ps, psk = banks[s]
                dk = dstk if outks is None else outks[s]
                act(sb_junk[:, 0:512], ps[:, 0:512], AF.Square, [psk], ["junk", ("ssq", sscol, s)],
                    accum=small[:, sscol + 4 * s + n: sscol + 4 * s + n + 1])
                S.op("dve", lambda e, s=s, ps=ps, n=n: e.tensor_copy(out=dst_fn(s)[:, n * 512:(n + 1) * 512], in_=ps[:, 0:512]),
                     reads=[psk], writes=dk)

    def finish_tok(buf, bufk, sscol, resid, residk):
        base = 10 if sscol < 20 else 20
        k = ("ssq", base, (sscol - base) // 4)
        ts(small[:, 30:31], small[:, sscol:sscol + 1], small[:, sscol + 1:sscol + 2], None, ALU.add, None, [k], ["fin"])
        ts(small[:, 30:31], small[:, 30:31], small[:, sscol + 2:sscol + 3], None, ALU.add, None, ["fin", k], ["fin"])
        ts(small[:, 30:31], small[:, 30:31], small[:, sscol + 3:sscol + 4], None, ALU.add, None, ["fin", k], ["fin"])
        rsqrt(small[:, 30:31], small[:, 30:31], 1.0 / D, ["fin"], ["fin"])
        stt(buf, buf, small[:, 30:31], gam[:, :], ALU.mult, ALU.mult, bufk + ["fin", "gam"], bufk)
        tt(buf, buf, resid, ALU.add, bufk + residk, bufk)

    if with_pass1:
        S.op("dve", lambda e: e.memset(LAM[:, :], 0.0), reads=[], writes=["LAM"])
        zero_state()
        n1 = n_own_mt + 1
        for m in range(n1):
            row0 = m * T
            xrows = lambda s, row0=row0: x1in[row0 + s * 128: row0 + (s + 1) * 128, :]
            norm_to_T(xrows, hT, "hT", "npre", True)
            if m == 0:
                slots1 = [dict(hist="zero"), dict(hist="prev", chain=True, masked=True), dict(hist="prev", chain=True),
                          dict(hist="prev", chain=True)]
            elif m < n1 - 1:
                slots1 = [dict(hist="carry", chain=True)] + [dict(hist="prev", chain=True)] * 3
            else:
                slots1 = [dict(hist="zero"), dict(hist="prev"), dict(hist="prev", chain=True), dict(hist="prev")]
            branchB_front(slots1, m < n1 - 2, False)
            for j, sl in enumerate(slots1):
                if sl.get("chain"):
                    ssd_slot(j, sl.get("masked", False), False, True)
        for g in range(8):
            dma("sp", cc_in.ap()[:, 512 * g:512 * g + 512], ST[:, 512 * g:512 * g + 512], [("ST", g)], ["ccin"], "ccp")
        dma("sp", cc_in.ap()[:, DI:DI + NH], LAM[:, :], ["LAM"], ["ccin"], "ccp")
        S.op("pool", lambda e: e.collective_compute("AllGather", ALU.bypass, replica_groups=[list(range(n_cores))],
                                                     ins=[cc_in.ap().opt()], outs=[cc_out.ap().opt()]),
             reads=["ccin"], writes=["ccout"])
        cco = cc_out.ap()
        dma("sp", V(lamall, n_cores * 64, 0, 128, 0, [[64, n_cores], [1, 64]]),
            cco[:, DI:DI + NH].rearrange("(r p) c -> p r c", p=128), ["ccout"], [("tmp", 0)], "cc2")
        for jj in range(n_cores):
            wjj = wj[:, jj * 64:(jj + 1) * 64]
            for m_ in range(n_cores):
                selc = pfc("sel", jj * 8 + m_)
                if m_ == 0:
                    ts(wjj, lamall[:, 0:64], selc, None, ALU.mult, None, [("tmp", 0), "pft"], [("tmp", 1)])
                else:
                    stt(wjj, lamall[:, m_ * 64:(m_ + 1) * 64], selc, wjj, ALU.mult, ALU.add, [("tmp", 0), "pft", ("tmp", 1)], [("tmp", 1)])
            act(wjj, wjj, AF.Exp, [("tmp", 1)], [("tmp", 1)])
            ts(wjj, wjj, pfc("selm", jj), None, ALU.mult, None, [("tmp", 1), "pft"], [("tmp", 1)])
        for g in range(8):
            S.op("dve", lambda e, g=g: e.memset(ST[:, 512 * g:512 * g + 512], 0.0), reads=[], writes=[("ST", g)])
        Pj = [xmid[:, 0:DI], RB[:, 0:2 * DI].bitcast(F32)]
        Pjk = [["xmid"], [RBk(p) for p in range(32)]]
        for jj in range(n_cores):
            pi = S.rot("Pj", 2)
            pb, pk = Pj[pi], Pjk[pi]
            dma("sp", pb, cco[jj * 128:(jj + 1) * 128, 0:DI], ["ccout"], pk, f"cc2_{pi}")
            for g in range(8):
                w_b = V(wj, n_cores * 64, 0, 128, jj * 64 + 8 * g, [[1, 8], [0, 64]])
                p3 = pb[:, 512 * g:512 * g + 512].rearrange("p (h q) -> p h q", h=8)
                tt(p3, p3, w_b, ALU.mult, pk + [("tmp", 1)], pk)
                tt(ST[:, 512 * g:512 * g + 512], ST[:, 512 * g:512 * g + 512], pb[:, 512 * g:512 * g + 512], ALU.add,
                   [("ST", g)] + pk, [("ST", g)])
        SIN = xmid[:, 0:DI]
        for g in range(8):
            act(SIN[:, 512 * g:512 * g + 512], ST[:, 512 * g:512 * g + 512], AF.Copy, [("ST", g), "xmid"], ["xmid"])

    mt0 = [dict(hist=("cache", 0), state=("load", 0), out=0),
           dict(hist=("cache", 1), state=("load", 1), out=1),
           dict(hist="zero"),
           dict(hist="prev", state=("halo",), masked=True)]
    def halo_state():
        if with_pass1:
            for g in range(8):
                act(ST[:, 512 * g:512 * g + 512], SIN[:, 512 * g:512 * g + 512], AF.Copy, ["xmid"], [("ST", g)])
                act(STb[:, 512 * g:512 * g + 512], ST[:, 512 * g:512 * g + 512], AF.Copy, [("ST", g)], [("STb", g)])
        else:
            zero_state()
    layer_mt(0, mt0, True, True)
    for m in range(n_own_mt):
        slots = [dict(hist="carry", state=("statein",))] + [dict(hist="prev", state=("statein",)) for _ in range(3)]
        if m == n_own_mt - 1:
            slots[3]["out"] = 2
        layer_mt((m + 1) * T, slots, m < n_own_mt - 1, False)

    allres = [k for k in S.dcnt]
    final_waits = [(("d", k), v) for k, v in S.dcnt.items()]

    sems = {}
    for e in S.streams:
        sems[e] = es.enter_context(nc.semaphore(f"s_{e}"))
    for k in S.dcnt:
        sems[("d", k)] = es.enter_context(nc.semaphore(f"d_{k}"))
    def clear_all(sp):
        for s_ in sems.values():
            sp.sem_clear(s_)

    with nc.Block() as b0:
        b0.sync(clear_all)
    block = nc.Block()
    block.__enter__()

    def replay(ename):
        def run(eng):
            for waits, fn, inc, dmak in S.streams[ename]:
                for k, v in waits[1:]:
                    eng.wait_ge(sems[k], v)
                ins = fn(eng)
                if waits:
                    ins._wait_ge(sems[waits[0][0]], waits[0][1])
                if dmak is not None:
                    ins.then_inc(sems[("d", dmak)], 16)
                elif inc:
                    ins.then_inc(sems[ename], 1)
            if ename == "sp":
                for k, v in final_waits:
                    eng.wait_ge(sems[k], v)
        return run

    block.tensor(replay("pe"))
    block.scalar(replay("act"))
    block.vector(replay("dve"))
    block.gpsimd(replay("pool"))
    block.sync(replay("sp"))
    block.__exit__(None, None, None)
    with nc.Block() as b2:
        b2.sync(clear_all)
    es.close()
    return nc


def _fm(v, nch):
    return np.ascontiguousarray(np.asarray(v, np.float32).reshape(nch, 128).T)


def _fm_rows(a, nch):
    a = np.asarray(a, np.float32)
    R = a.shape[0]
    return np.ascontiguousarray(a.reshape(R, nch, 128).transpose(2, 1, 0).reshape(128, nch * R))


def _unfm_rows(b, nch, R):
    return np.ascontiguousarray(b.reshape(128, nch, R).transpose(2, 1, 0).reshape(R, nch * 128))


def _ws_tiles(w, starts, nkc):
    w = np.asarray(w, np.float32)
    out = np.empty((len(starts), 128, nkc * 128), np.float32)
    for i, s in enumerate(starts):
        out[i] = w[:nkc * 128, s:s + 128].reshape(nkc, 128, 128).transpose(1, 0, 2).reshape(128, nkc * 128)
    return out


def _tm_tiles(w, nkc, piece):
    w = np.asarray(w, np.float32)
    npiece = nkc // piece
    out = np.empty((4 * npiece, 128, piece * 512), np.float32)
    for n in range(4):
        for pc in range(npiece):
            blk = w[pc * piece * 128:(pc + 1) * piece * 128, n * 512:(n + 1) * 512]
            out[n * npiece + pc] = blk.reshape(piece, 128, 512).transpose(1, 0, 2).reshape(128, piece * 512)
    return out


def _tile_weights(p):
    starts = ([C_AVAL + 128 * i for i in range(16)] + [C_AGATE + 128 * i for i in range(16)]
              + [C_Z + 128 * i for i in range(32)] + [C_XBC + 128 * i for i in range(48)] + [C_DT]
              + [C_GA + 128 * i for i in range(16)] + [C_GB + 128 * i for i in range(16)])
    w_in = np.asarray(p["w_in"][0], np.float32)
    w_in_pad = np.concatenate([w_in, np.zeros((D, 128), np.float32)], 1)
    return {
        "w_in_t": _ws_tiles(w_in_pad, starts, KC),
        "w_a_t": _ws_tiles(p["w_a_out"][0], [128 * i for i in range(16)], KC),
        "w_b_t": _ws_tiles(p["w_b_out"][0], [128 * i for i in range(16)], 32),
        "w_o_t": _tm_tiles(p["w_o"][0], KC, 4),
        "w_up_t": _ws_tiles(p["w_up"][0], [128 * i for i in range(88)], KC),
        "w_dn_t": _tm_tiles(p["w_down"][0], NJ, 4),
    }


def _consts():
    cs = np.zeros((128, NCS), np.float32)
    cs[:, 0:128] = np.eye(128, dtype=np.float32)
    s = np.arange(64)
    cs[0:64, 128:192] = (s[:, None] <= s[None, :]).astype(np.float32)
    cs[0:64, 192:256] = np.where(s[None, :] >= s[:, None], 0.0, NEG).astype(np.float32)
    cs[:, 256:384] = 1.0
    return cs


def _pack_params(p, k, n_cores):
    pf = np.zeros((128, NPF), np.float32)

    def put(name, arr):
        o, w = PF[name]
        arr = np.asarray(arr, np.float32)
        pf[0:arr.shape[0], o:o + w] = arr.reshape(arr.shape[0], w)

    put("npre", _fm(p["norm_mix_pre"][0], 16))
    put("caw", _fm_rows(p["conv_a_w"][0], 16))
    put("cab", _fm(p["conv_a_b"][0], 16))
    put("lng", _fm(p["ln_a_g"][0], 16))
    put("lnb", _fm(p["ln_a_b"][0], 16))
    put("bg", _fm(p["b_gate"][0], 32))
    put("cbw", _fm_rows(p["conv_b_w"][0], 48))
    put("cbb", _fm(p["conv_b_b"][0], 48))
    put("sng", _fm(p["ssd_norm_g"][0], 32))
    put("nfpre", _fm(p["norm_ffn_pre"][0], 16))
    put("fcw", _fm_rows(p["ffn_conv_w"][0], 88))
    put("fcb", _fm(p["ffn_conv_b"][0], 88))
    dsk = np.asarray(p["d_skip"][0], np.float32)
    d2 = np.zeros((128, 32), np.float32)
    for c in range(32):
        d2[0:64, c] = dsk[2 * c]
        d2[64:128, c] = dsk[2 * c + 1]
    put("dsk", d2)
    put("dtb", np.asarray(p["dt_bias"][0], np.float32).reshape(64, 1))
    put("alog", np.asarray(p["a_log"][0], np.float32).reshape(64, 1))
    put("mk", np.full((128, 1), 0.0 if k == 0 else 1.0, np.float32))
    sel = np.zeros((8, 8), np.float32)
    selm = np.zeros((8,), np.float32)
    for j in range(8):
        if j < k:
            selm[j] = 1.0
            for m in range(8):
                if j < m < k:
                    sel[j, m] = 1.0
    put("sel", np.broadcast_to(sel.reshape(1, 64), (128, 64)))
    put("selm", np.broadcast_to(selm.reshape(1, 8), (128, 8)))
    return pf


def _core_inputs(p, k, n_own_mt, with_pass1, n_cores, own_tokens):
    xp = np.asarray(p["x_prompt"], np.float32)[0]
    xs = np.asarray(p["x_sample"], np.float32)
    s0 = k * own_tokens

    def prows(a, b):
        out = np.zeros((b - a, D), np.float32)
        lo, hi = max(a, 0), max(b, 0)
        if hi > lo:
            out[lo - a:hi - a] = xp[lo:hi]
        return out

    xin = np.concatenate([xs[2 * k], xs[2 * k + 1], prows(s0 - 128, s0), prows(s0, s0 + own_tokens)], 0)
    m = {"xin": np.ascontiguousarray(xin)}
    if with_pass1:
        n1 = n_own_mt + 1
        e0 = s0 + own_tokens
        parts = [prows(s0 - 128, s0 + 128)]
        if n1 > 2:
            parts.append(prows(s0 + 128, s0 + 128 + 256 * (n1 - 2)))
        parts.append(prows(e0 - 256, e0))
        m["x1in"] = np.ascontiguousarray(np.concatenate(parts, 0))
    m["pf"] = _pack_params(p, k, n_cores)
    m["uh"] = np.stack([_fm_rows(p["cache_conv_a"][0, 2 * k + i], 16) for i in range(2)])
    m["xh"] = np.stack([_fm_rows(p["cache_conv_b"][0, 2 * k + i], 48) for i in range(2)])
    m["fh"] = np.stack([_fm_rows(p["cache_ffn_conv"][0, 2 * k + i], 88) for i in range(2)])
    m["st"] = np.stack([np.ascontiguousarray(np.asarray(p["state_ssd"][0, 2 * k + i], np.float32).reshape(DI, 128).T)
                        for i in range(2)])
    return m


_NC_CACHE = {}
LAST_RES = None


def run_cores(p, n_cores, n_own_mt, with_pass1):
    own_tokens = n_own_mt * T
    key = (n_cores, n_own_mt, with_pass1)
    if key not in _NC_CACHE:
        _NC_CACHE[key] = build(n_own_mt, with_pass1, n_cores)
    nc = _NC_CACHE[key]
    shared = {
        "cs": _consts(),
        "gam1": np.ascontiguousarray(np.broadcast_to(np.asarray(p["norm_mix_post"][0], np.float32)[None, :], (128, D))),
        "gam2": np.ascontiguousarray(np.broadcast_to(np.asarray(p["norm_ffn_post"][0], np.float32)[None, :], (128, D))),
    }
    shared.update(_tile_weights(p))
    in_maps = []
    for k in range(n_cores):
        m = _core_inputs(p, k, n_own_mt, with_pass1, n_cores, own_tokens)
        m.update(shared)
        in_maps.append(m)
    if os.environ.get("KTRACE", "0") == "1":
        res = run_bass_kernel_spmd(nc, in_maps, core_ids=list(range(n_cores)), trace=True)
        print("KTRACE exec_time_ns", res.exec_time_ns)
        global LAST_RES
        LAST_RES = res
    else:
        res = run_bass_kernel_spmd(nc, in_maps, core_ids=list(range(n_cores)))
    return res.results


def assemble(results, n_cores, own_tokens):
    nseq = 2 * n_cores
    y_prompt = np.zeros((1, n_cores * own_tokens, D), np.float32)
    y_sample = np.zeros((nseq, 64, D), np.float32)
    ca_s = np.zeros((1, nseq, 30, D), np.float32)
    cb_s = np.zeros((1, nseq, 3, DXBC), np.float32)
    ss_s = np.zeros((1, nseq, NH, 64, 128), np.float32)
    cf_s = np.zeros((1, nseq, 2, 2 * DFF), np.float32)
    for k, r in enumerate(results):
        yo = np.asarray(r["yout"]).reshape(-1, D)
        r = dict(r)
        for nm, w in (("ouh", 480), ("oxh", 144), ("ofh", 176)):
            r[nm] = np.asarray(r[nm]).reshape(3, 128, w)
        y_sample[2 * k] = yo[0:64]
        y_sample[2 * k + 1] = yo[64:128]
        y_prompt[0, k * own_tokens:(k + 1) * own_tokens] = yo[256:256 + own_tokens]
        for i in range(2):
            ca_s[0, 2 * k + i] = _unfm_rows(r["ouh"][i], 16, 30)
            cb_s[0, 2 * k + i] = _unfm_rows(r["oxh"][i], 48, 3)
            cf_s[0, 2 * k + i] = _unfm_rows(r["ofh"][i], 88, 2)
            ss_s[0, 2 * k + i] = np.asarray(r["ost"]).reshape(3, 128, DI)[i].T.reshape(NH, 64, 128)
    r = dict(results[-1])
    for nm, w in (("ouh", 480), ("oxh", 144), ("ofh", 176)):
        r[nm] = np.asarray(r[nm]).reshape(3, 128, w)
    ca_p = _unfm_rows(r["ouh"][2], 16, 30)[None, None]
    cb_p = _unfm_rows(r["oxh"][2], 48, 3)[None, None]
    cf_p = _unfm_rows(r["ofh"][2], 88, 2)[None, None]
    ss_p = np.ascontiguousarray(np.asarray(r["ost"]).reshape(3, 128, DI)[2].T.reshape(NH, 64, 128))[None, None]
    return (y_prompt, y_sample, ca_p, cb_p, ss_p, cf_p, ca_s, cb_s, ss_s, cf_s)


def kernel(**inputs):
    n_cores = 8
    n_own_mt = 2048 // T
    results = run_cores(inputs, n_cores, n_own_mt, True)
    return assemble(results, n_cores, n_own_mt * T)
```
